# Optimizing a Trainium2 kernel written in Bass

```python
import jax, jax.numpy as jnp
from jax import lax
import numpy as np

D_MODEL = 1024
BATCH = 2
SEQ = 8192
DEPTH = 1

CHUNK = 64
N_LEFT_CHUNKS = 8
BAND = (N_LEFT_CHUNKS + 1) * CHUNK
MIX_WIDTH = D_MODEL
CONV_WIDTH = MIX_WIDTH // 2
ATTN_WIDTH = MIX_WIDTH - CONV_WIDTH
HEAD_DIM = 64
N_HEADS = ATTN_WIDTH // HEAD_DIM
CONV_KERNEL = 31
MAX_REL = 128
D_FF = 2816
FFN_CONV_KERNEL = 3
IN_COLS = 2 * CONV_WIDTH + 3 * ATTN_WIDTH
EPS = 1e-6
NEG_INF = -1e30

kernel_name = "chunk_causal_conformer_hybrid_block"


def rms_norm(x, g):
    xf = x.astype(jnp.float32)
    y = xf * lax.rsqrt(jnp.mean(xf * xf, axis=-1, keepdims=True) + EPS)
    return (y * g.astype(jnp.float32)).astype(x.dtype)


def layer_norm(x, g, b):
    xf = x.astype(jnp.float32)
    mu = jnp.mean(xf, axis=-1, keepdims=True)
    xc = xf - mu
    var = jnp.mean(xc * xc, axis=-1, keepdims=True)
    y = xc * lax.rsqrt(var + EPS) * g.astype(jnp.float32) + b.astype(jnp.float32)
    return y.astype(x.dtype)


def causal_depthwise_conv(x, w, b):
    k = w.shape[0]
    c = x.shape[-1]
    y = lax.conv_general_dilated(
        x, w[:, None, :].astype(x.dtype), window_strides=(1,), padding=[(k - 1, 0)],
        dimension_numbers=("NWC", "WIO", "NWC"), feature_group_count=c)
    return y + b.astype(x.dtype)


def conformer_conv_group(a_val, a_gate, dw_w, dw_b, ln_g, ln_b):
    h = a_val * jax.nn.sigmoid(a_gate)
    h = causal_depthwise_conv(h, dw_w, dw_b)
    h = layer_norm(h, ln_g, ln_b)
    return jax.nn.silu(h)


def rel_bias_band(rel_table):
    i = np.arange(CHUNK)[:, None]
    j = np.arange(BAND)[None, :]
    rel = N_LEFT_CHUNKS * CHUNK + i - j
    idx = np.clip(rel, -MAX_REL, MAX_REL) + MAX_REL
    return rel_table[:, idx]


def chunked_band_attention(q, k, v, rel_table):
    b, t, _ = q.shape
    nc = t // CHUNK
    pad = N_LEFT_CHUNKS * CHUNK
    q = q.reshape(b, nc, CHUNK, N_HEADS, HEAD_DIM)

    def band(z):
        z = z.reshape(b, t, N_HEADS, HEAD_DIM)
        z = jnp.pad(z, ((0, 0), (pad, 0), (0, 0), (0, 0)))
        zc = z.reshape(b, nc + N_LEFT_CHUNKS, CHUNK, N_HEADS, HEAD_DIM)
        return jnp.concatenate([zc[:, s:s + nc] for s in range(N_LEFT_CHUNKS + 1)], axis=2)

    kb = band(k)
    vb = band(v)
    scale = HEAD_DIM ** -0.5
    scores = jnp.einsum("bnqhd,bnkhd->bhnqk", q, kb,
                        preferred_element_type=jnp.float32) * scale
    scores = scores + rel_bias_band(rel_table).astype(jnp.float32)[None, :, None]
    key_pos = (jnp.arange(nc)[:, None] - N_LEFT_CHUNKS) * CHUNK + jnp.arange(BAND)[None, :]
    valid = (key_pos >= 0)[None, None, :, None, :]
    scores = jnp.where(valid, scores, NEG_INF)
    p = jax.nn.softmax(scores, axis=-1)
    out = jnp.einsum("bhnqk,bnkhd->bnqhd", p.astype(vb.dtype), vb)
    return out.reshape(b, t, ATTN_WIDTH)


def conv_gated_ffn(u, w_up, dw_w, dw_b, w_down):
    h = u @ w_up
    h = causal_depthwise_conv(h, dw_w, dw_b)
    gate, val = jnp.split(h, 2, axis=-1)
    return (jax.nn.gelu(gate) * val) @ w_down


def setup_inputs(seed: int = 0) -> dict:
    key = jax.random.key(seed)
    ks = jax.random.split(key, 20)
    L = DEPTH
    nrm = jax.random.normal
    f32 = jnp.float32
    return {
        "x": nrm(ks[0], (BATCH, SEQ, D_MODEL), f32),
        "norm_mix_pre": 1.0 + 0.05 * nrm(ks[1], (L, D_MODEL), f32),
        "w_in": nrm(ks[2], (L, D_MODEL, IN_COLS), f32) * D_MODEL ** -0.5,
        "conv_dw_w": nrm(ks[3], (L, CONV_KERNEL, CONV_WIDTH), f32) * CONV_KERNEL ** -0.5,
        "conv_dw_b": 0.01 * nrm(ks[4], (L, CONV_WIDTH), f32),
        "conv_ln_g": 1.0 + 0.05 * nrm(ks[5], (L, CONV_WIDTH), f32),
        "conv_ln_b": 0.01 * nrm(ks[6], (L, CONV_WIDTH), f32),
        "rel_bias": 0.5 * nrm(ks[7], (L, N_HEADS, 2 * MAX_REL + 1), f32),
        "w_out": nrm(ks[8], (L, MIX_WIDTH, D_MODEL), f32) * MIX_WIDTH ** -0.5,
        "norm_mix_post": 1.0 + 0.05 * nrm(ks[9], (L, D_MODEL), f32),
        "norm_ffn_pre": 1.0 + 0.05 * nrm(ks[10], (L, D_MODEL), f32),
        "w_up": nrm(ks[11], (L, D_MODEL, 2 * D_FF), f32) * D_MODEL ** -0.5,
        "ffn_dw_w": nrm(ks[12], (L, FFN_CONV_KERNEL, 2 * D_FF), f32) * FFN_CONV_KERNEL ** -0.5,
        "ffn_dw_b": 0.01 * nrm(ks[13], (L, 2 * D_FF), f32),
        "w_down": nrm(ks[14], (L, D_FF, D_MODEL), f32) * D_FF ** -0.5,
        "norm_ffn_post": 1.0 + 0.05 * nrm(ks[15], (L, D_MODEL), f32),
    }


def reference(x, norm_mix_pre, w_in, conv_dw_w, conv_dw_b, conv_ln_g, conv_ln_b,
              rel_bias, w_out, norm_mix_post, norm_ffn_pre, w_up, ffn_dw_w, ffn_dw_b,
              w_down, norm_ffn_post):
    h = x
    for l in range(DEPTH):
        u = rms_norm(h, norm_mix_pre[l])
        proj = u @ w_in[l]
        a_val, a_gate, q, k, v = jnp.split(
            proj, np.cumsum([CONV_WIDTH, CONV_WIDTH, ATTN_WIDTH, ATTN_WIDTH]).tolist(), axis=-1)
        conv_out = conformer_conv_group(a_val, a_gate, conv_dw_w[l], conv_dw_b[l],
                                        conv_ln_g[l], conv_ln_b[l])
        attn_out = chunked_band_attention(q, k, v, rel_bias[l])
        mixed = jnp.concatenate([conv_out, attn_out], axis=-1) @ w_out[l]
        h = h + rms_norm(mixed, norm_mix_post[l])
        u = rms_norm(h, norm_ffn_pre[l])
        f = conv_gated_ffn(u, w_up[l], ffn_dw_w[l], ffn_dw_b[l], w_down[l])
        h = h + rms_norm(f, norm_ffn_post[l])
    return h
```

```python
import numpy as np
from contextlib import ExitStack
import concourse.bass as bass
import concourse.mybir as mybir
from concourse.bass_utils import run_bass_kernel_spmd

F32 = mybir.dt.float32
BF16 = mybir.dt.bfloat16
AF = mybir.ActivationFunctionType
ALU = mybir.AluOpType

D = 1024
NT = 21
NQ = 17
QC = NQ * 128
OWN = 2048
DFF = 2816
EPS = 1e-6
DEBUG = False


class Sched:
    def __init__(self, nc, es):
        self.nc = nc
        self.es = es
        self.eng = {"pe": nc.tensor, "act": nc.scalar, "dve": nc.vector,
                    "pool": nc.gpsimd, "sp": nc.sync}
        self.sem = {e: es.enter_context(nc.semaphore("s_" + e)) for e in self.eng}
        self.cnt = {e: 0 for e in self.eng}
        self.known = {e: {} for e in self.eng}
        self.res = {}
        self.dsem = {}
        self.ninst = {e: 0 for e in self.eng}

    def _need(self, e, tok, out):
        if tok is None:
            return
        sem, name, val, peng = tok
        if peng == "pe" and e == "pe":
            return
        if self.known[e].get(name, 0) >= val:
            return
        cur = out.get(name)
        if cur is None or cur[2] < val:
            out[name] = tok

    def _emit_waits(self, e, need):
        for name, tok in need.items():
            self.eng[e].wait_ge(tok[0], tok[2])
            self.known[e][name] = tok[2]

    def _deps(self, e, reads, writes):
        need = {}
        for r in reads:
            st = self.res.get(r)
            if st is not None:
                self._need(e, st["w"], need)
        for w in writes:
            st = self.res.get(w)
            if st is not None:
                self._need(e, st["w"], need)
                for t in st["r"].values():
                    self._need(e, t, need)
        self._emit_waits(e, need)

    def _record(self, tok, reads, writes):
        for r in reads:
            st = self.res.setdefault(r, {"w": None, "r": {}})
            cur = st["r"].get(tok[1])
            if cur is None or cur[2] < tok[2]:
                st["r"][tok[1]] = tok
        for w in writes:
            self.res[w] = {"w": tok, "r": {}}

    def op(self, e, fn, reads=(), writes=()):
        self._deps(e, reads, writes)
        ins = fn()
        self.cnt[e] += 1
        ins.then_inc(self.sem[e], 1)
        tok = (self.sem[e], "s_" + e, self.cnt[e], e)
        self._record(tok, reads, writes)
        self.ninst[e] += 1
        return tok

    def mm(self, fn, reads=(), writes=(), last=True):
        e = "pe"
        self._deps(e, reads, writes)
        ins = fn()
        self.ninst[e] += 1
        if last:
            self.cnt[e] += 1
            ins.then_inc(self.sem[e], 1)
            tok = (self.sem[e], "s_pe", self.cnt[e], e)
        else:
            tok = (self.sem[e], "s_pe", self.cnt[e] + 1, e)
        self._record(tok, reads, writes)
        return tok

    def dma(self, q, slot, fn, reads=(), writes=()):
        if slot not in self.dsem:
            self.dsem[slot] = [self.es.enter_context(self.nc.semaphore("d_" + slot)), 0]
        self._deps(q, reads, writes)
        inss = fn()
        if not isinstance(inss, (list, tuple)):
            inss = [inss]
        ds = self.dsem[slot]
        for ins in inss:
            ins.then_inc(ds[0], 16)
            ds[1] += 16
        tok = (ds[0], "d_" + slot, ds[1], None)
        self._record(tok, reads, writes)
        return tok

    def fence(self, engines=("pe", "act", "dve", "pool", "sp")):
        toks = []
        for f in self.eng:
            if self.cnt[f] > 0:
                toks.append((self.sem[f], "s_" + f, self.cnt[f], f))
        for slot, ds in self.dsem.items():
            if ds[1] > 0:
                toks.append((ds[0], "d_" + slot, ds[1], None))
        for e in engines:
            need = {}
            for t in toks:
                if t[3] == e:
                    continue
                self._need(e, t, need)
            self._emit_waits(e, need)


def build_program():
    nc = bass.Bass("TRN2", target_bir_lowering=False)
    dr = lambda name, shape, dt=F32, kind="ExternalInput": nc.dram_tensor(name, shape, dt, kind=kind).ap()
    xw = dr("xw", [NT * 128, D])
    flag_d = dr("flag", [128, 1])
    w_in_d = dr("w_in", [D, 2560])
    w_out_d = dr("w_out", [D, D])
    w_up_d = dr("w_up", [D, 2 * DFF])
    w_down_d = dr("w_down", [DFF, D])
    gvec_d = dr("gvec", [4, D])
    cw_d = dr("cw", [128, 4 * 31])
    cvec_d = dr("cvec", [128, 12])
    fdw_d = dr("fdw", [128, 44 * 4])
    biasT_d = dr("biasT", [8, 128, 640])
    maskT_d = dr("maskT", [128, 640])
    ident_d = dr("ident", [128, 128])
    out_d = dr("out", [OWN, D], F32, "ExternalOutput")
    h1s = nc.dram_tensor("h1s", [OWN, D], F32).ap()
    dbg = {}

    with ExitStack() as es:
        S = Sched(nc, es)
        big = es.enter_context(nc.sbuf_tensor("big", [128, 51200], F32))

        def view(off, shape, dt):
            n = int(np.prod(shape))
            esz = 4 if dt == F32 else 2
            nb = n * esz
            assert off % 4 == 0 and nb % 4 == 0 and off + nb <= 204800, (off, nb)
            ap = big[:, off // 4:(off + nb) // 4]
            if dt != F32:
                ap = ap.bitcast(dt)
            if len(shape) == 2:
                ap = ap.rearrange("p (a b) -> p a b", a=shape[0])
            elif len(shape) == 3:
                ap = ap.rearrange("p (a b c) -> p a b c", a=shape[0], b=shape[1])
            return ap

        def dump(name, ap, shape, dt, reads):
            if not DEBUG:
                return
            t = nc.dram_tensor("dbg_" + name, [128] + list(shape), dt, kind="ExternalOutput").ap()
            dbg[name] = t
            S.dma("sp", "dbg_" + name, lambda: nc.sync.dma_start(out=t, in_=ap), reads=reads)

        o = 0
        ident_f = view(o, [128], F32); o += 512
        ident_b = view(o, [128], BF16); o += 256
        ones_b = view(o, [128], BF16); o += 256
        gb = [view(o + i * 4096, [1024], F32) for i in range(4)]; o += 16384
        cw = view(o, [4, 31], F32); o += 496
        cvec = view(o, [4, 3], F32); o += 48
        fdw = view(o, [44, 4], F32); o += 704
        flag = view(o, [1], F32); o += 4
        stat = view(o, [448], F32); o += 1792
        nh1 = view(o, [1], F32); o += 4
        assert o <= 20480
        stat_i = [0]

        def newstat(n=1):
            i = stat_i[0]
            stat_i[0] += n
            assert stat_i[0] <= 448
            return i

        xt_ring = [view(166528 + i * 4096, [1024], F32) for i in range(4)]
        for w0 in range(4):
            S.dma("sp", "xt%d" % w0, lambda w0=w0: nc.sync.dma_start(out=xt_ring[w0], in_=xw[w0 * 128:(w0 + 1) * 128, :]),
                  writes=[("xt", w0)])
        S.dma("sp", "c_id", lambda: nc.sync.dma_start(out=ident_f, in_=ident_d), writes=["ident_f"])
        S.dma("sp", "c_flag", lambda: nc.sync.dma_start(out=flag, in_=flag_d), writes=["flag"])
        S.dma("act", "c_g0", lambda: nc.scalar.dma_start(out=gb[0], in_=gvec_d[0:1, :].partition_broadcast(128)),
              writes=[("gb", 0)])
        S.dma("sp", "c_cw", lambda: nc.sync.dma_start(out=cw, in_=cw_d.rearrange("p (a b) -> p a b", a=4)), writes=["cw"])
        S.dma("sp", "c_cvec", lambda: nc.sync.dma_start(out=cvec, in_=cvec_d.rearrange("p (a b) -> p a b", a=4)), writes=["cvec"])
        S.dma("sp", "c_fdw", lambda: nc.sync.dma_start(out=fdw, in_=fdw_d.rearrange("p (a b) -> p a b", a=44)), writes=["fdw"])
        S.op("dve", lambda: nc.vector.tensor_copy(out=ident_b, in_=ident_f), reads=["ident_f"], writes=["ident_b"])
        S.op("dve", lambda: nc.vector.memset(ones_b, 1.0), writes=["ones_b"])

        R_U = 20480
        R_CAT = 53312
        R_KQV = 88128
        R_HG = 148880
        R_T = 166528
        u2T = view(R_U, [8, 2052], BF16)
        conv_outT = view(R_CAT, [4, QC], BF16)
        attn_outT = view(R_CAT + 17408, [4, QC], BF16)
        kT = view(R_KQV, [4, NT * 128], BF16)
        qT = view(R_KQV + 21504, [4, QC], BF16)
        V1 = view(R_KQV + 21504 + 17408, [NT, 8, 65], BF16)
        hglu = view(R_HG, [4, QC + 30], BF16)

        def rstd_from(ssq_col, out_col, n_feat, reads):
            tmp = newstat()
            S.op("pool", lambda: nc.gpsimd.tensor_scalar(out=stat[:, tmp:tmp + 1], in0=stat[:, ssq_col:ssq_col + 1],
                                                        scalar1=1.0 / n_feat, scalar2=EPS, op0=ALU.mult, op1=ALU.add),
                 reads=reads, writes=[("st", tmp)])
            S.op("pool", lambda: nc.gpsimd.tensor_tensor(out=stat[:, out_col:out_col + 1], in0=stat[:, tmp:tmp + 1],
                                                        in1=nh1, op=ALU.pow),
                 reads=[("st", tmp), "nh"], writes=[("st", out_col)])

        def rstd_act(ssq_col, out_col, n_feat, reads):
            tmp = newstat()
            S.op("act", lambda: nc.scalar.activation(out=stat[:, tmp:tmp + 1], in_=stat[:, ssq_col:ssq_col + 1], func=AF.Ln,
                                                     scale=1.0 / n_feat, bias=EPS),
                 reads=reads, writes=[("st", tmp)])
            S.op("act", lambda: nc.scalar.activation(out=stat[:, out_col:out_col + 1], in_=stat[:, tmp:tmp + 1], func=AF.Exp,
                                                     scale=-0.5),
                 reads=[("st", tmp)], writes=[("st", out_col)])

        with ExitStack() as psA:
            PP = [psA.enter_context(nc.psum_tensor("pp%d" % i, [128, 2, 512], F32)) for i in range(3)]
            pf = [PP[i // 2][:, i % 2, :] for i in range(6)]
            ptb2 = psA.enter_context(nc.psum_tensor("ptb2", [128, 2, 1024], BF16))
            ptb = [ptb2[:, i, :] for i in range(2)]
            pfi = [0]

            def nbank():
                i = pfi[0] % 6
                pfi[0] += 1
                return i

            W_in = view(R_U, [8, 2560], BF16)
            uT_ring = [view(R_U + 40960 + i * 8192, [8, 512], BF16) for i in range(3)]
            ub_ring = [view(R_T + 16384 + i * 2048, [1024], BF16) for i in range(4)]
            sig_ring = [view(R_T + 24576 + i * 2048, [512], F32) for i in range(2)]
            junk = view(R_T + 28672, [1024], BF16)
            S.op("pool", lambda: nc.gpsimd.memset(nh1, -0.5), writes=["nh"])
            w_in_v = w_in_d.rearrange("(k p) c -> p k c", p=128)
            for k in range(8):
                S.dma("pool", "w_in%d" % k, lambda k=k: nc.gpsimd.dma_start(out=W_in[:, k, 1536:2560], in_=w_in_v[:, k, 1536:2560]),
                      writes=[("W_in", k, 0)])
            def a1_late_setup():
                for k in range(8):
                    S.dma("pool", "w_inb%d" % k, lambda k=k: nc.gpsimd.dma_start(out=W_in[:, k, 0:1536], in_=w_in_v[:, k, 0:1536]),
                          writes=[("W_in", k, 1)])
                S.op("pool", lambda: nc.gpsimd.memset(hglu[:, :, 0:30], 0.0), writes=["hglu_pad"])
                S.op("pool", lambda: nc.gpsimd.memset(V1[:, :, :, 64:65], 1.0), writes=["V1ones"])
                S.op("pool", lambda: nc.gpsimd.tensor_scalar(out=V1[:, 0:5, :, 64:65], in0=V1[:, 0:5, :, 64:65],
                                                            scalar1=flag[:, 0:1], scalar2=1e-30, op0=ALU.mult, op1=ALU.max),
                     reads=["flag", "V1ones"], writes=["V1ones"])


            groups = [(0, 4, False), (4, 1, True), (5, 4, True), (9, 4, True), (13, 4, True), (17, 4, True)]
            evac_flip = [0]
            tile_info = []
            for gi, (t0, nt, full) in enumerate(groups):
                for j in range(nt):
                    tile_info.append((gi, j, t0 + j, j == nt - 1))

            def a1_dma(ti):
                gi, j, w, lastj = tile_info[ti]
                if w < 4:
                    return
                xt = xt_ring[w % 4]
                S.dma("sp", "xt%d" % (w % 4), lambda: nc.sync.dma_start(out=xt, in_=xw[w * 128:(w + 1) * 128, :]),
                      writes=[("xt", w % 4)])

            def a1_s1a(ti):
                gi, j, w, lastj = tile_info[ti]
                xt = xt_ring[w % 4]; xk = ("xt", w % 4)
                ub = ub_ring[w % 4]; ubk = ("ub", w % 4)
                c_ssq, c_r = newstat(), newstat()
                S.op("act", lambda: nc.scalar.activation(out=junk, in_=xt, func=AF.Square, accum_out=stat[:, c_ssq:c_ssq + 1]),
                     reads=[xk], writes=["junk", ("st", c_ssq)])
                rstd_from(c_ssq, c_r, D, [("st", c_ssq)])
                S.op("dve", lambda: nc.vector.scalar_tensor_tensor(out=ub, in0=xt, scalar=stat[:, c_r:c_r + 1], in1=gb[0],
                                                                   op0=ALU.mult, op1=ALU.mult),
                     reads=[xk, ("st", c_r), ("gb", 0)], writes=[ubk])

            def a1_s1b(ti):
                gi, j, w, lastj = tile_info[ti]
                ub = ub_ring[w % 4]; ubk = ("ub", w % 4)
                pt = ptb[w % 2]; ptk = ("ptb", w % 2)
                for k in range(8):
                    S.mm(lambda k=k: nc.tensor.transpose(out=pt[:, k * 128:(k + 1) * 128], in_=ub[:, k * 128:(k + 1) * 128],
                                                         identity=ident_b),
                         reads=[ubk, "ident_b"], writes=[ptk], last=(k == 7))

            def a1_s2(ti):
                gi, j, w, lastj = tile_info[ti]
                ut = uT_ring[gi % 3]
                pt = ptb[w % 2]; ptk = ("ptb", w % 2)
                S.op("act", lambda: nc.scalar.copy(out=ut[:, :, j * 128:(j + 1) * 128],
                                                   in_=pt[:, :].rearrange("p (k t) -> p k t", k=8)),
                     reads=[ptk], writes=[("uT", gi % 3)])

            def a1_proj(gi, hooks):
                t0, nt, full = groups[gi]
                N = nt * 128
                ut = uT_ring[gi % 3]
                utk = ("uT", gi % 3)
                ct0 = (t0 - 4) * 128

                nblk = [0]

                def maybe_hook():
                    nblk[0] += 1
                    if hooks and nblk[0] > 2:
                        hooks.pop(0)()

                def proj_block(col0):
                    maybe_hook()
                    b = nbank()
                    for k in range(8):
                        S.mm(lambda k=k: nc.tensor.matmul(pf[b][:, :N], lhsT=W_in[:, k, col0:col0 + 128],
                                                          rhs=ut[:, k, :N], start=(k == 0), stop=(k == 7)),
                             reads=[("W_in", k, 0 if col0 >= 1536 else 1), utk], writes=[("pf", b)], last=(k == 7))
                    return b

                if full:
                    for c in range(4):
                        bg = proj_block((4 + c) * 128)
                        sg = sig_ring[c % 2]; sgk = ("sig", c % 2)
                        S.op("act", lambda: nc.scalar.activation(out=sg[:, :N], in_=pf[bg][:, :N], func=AF.Sigmoid),
                             reads=[("pf", bg)], writes=[sgk])
                        bv = proj_block(c * 128)
                        S.op("dve", lambda: nc.vector.tensor_tensor(out=hglu[:, c, 30 + ct0:30 + ct0 + N], in0=pf[bv][:, :N],
                                                                    in1=sg[:, :N], op=ALU.mult),
                             reads=[("pf", bv), sgk], writes=[("hglu", c, gi)])
                    for c in range(4):
                        b = proj_block((8 + c) * 128)
                        S.op("act", lambda: nc.scalar.mul(out=qT[:, c, ct0:ct0 + N], in_=pf[b][:, :N], mul=0.125),
                             reads=[("pf", b)], writes=[("qT", c, gi)])
                for c in range(4):
                    b = proj_block((12 + c) * 128)
                    S.op("dve", lambda: nc.vector.tensor_copy(out=kT[:, c, t0 * 128:t0 * 128 + N], in_=pf[b][:, :N]),
                         reads=[("pf", b)], writes=[("kT", c, gi)])
                for j in range(nt):
                    w = t0 + j
                    maybe_hook()
                    b = nbank()
                    for k in range(8):
                        S.mm(lambda k=k: nc.tensor.matmul(pf[b][:, :512], lhsT=ut[:, k, j * 128:(j + 1) * 128],
                                                          rhs=W_in[:, k, 2048:2560], start=(k == 0), stop=(k == 7)),
                             reads=[("W_in", k, 0), utk], writes=[("pf", b)], last=(k == 7))
                    src = pf[b][:, :512].rearrange("p (h d) -> p h d", h=8)
                    if evac_flip[0] % 2 == 0:
                        S.op("act", lambda: nc.scalar.copy(out=V1[:, w, :, 0:64], in_=src), reads=[("pf", b)], writes=[("V1", w)])
                    else:
                        S.op("dve", lambda: nc.vector.tensor_copy(out=V1[:, w, :, 0:64], in_=src), reads=[("pf", b)], writes=[("V1", w)])
                    evac_flip[0] += 1

            tiles_of = {}
            for ti, (gi, j, w, lastj) in enumerate(tile_info):
                tiles_of.setdefault(gi, []).append(ti)
            ng = len(groups)
            for ti in tiles_of[0]:
                a1_dma(ti)
            for ti in tiles_of[0]:
                a1_s1a(ti)
            a1_late_setup()
            for ti in tiles_of[1]:
                a1_dma(ti)
            for ti in tiles_of[0]:
                a1_s1b(ti)
                a1_s2(ti)
            for ti in tiles_of[1]:
                a1_s1a(ti)
            for g in range(ng):
                hooks = []
                if g + 1 < ng:
                    for ti in tiles_of[g + 1]:
                        hooks.append(lambda ti=ti: a1_s1b(ti))
                        hooks.append(lambda ti=ti: a1_s2(ti))
                if g + 2 < ng:
                    for ti in tiles_of[g + 2]:
                        a1_dma(ti)
                a1_proj(g, hooks)
                while hooks:
                    hooks.pop(0)()
                if g + 2 < ng:
                    for ti in tiles_of[g + 2]:
                        a1_s1a(ti)
            dump("kT", kT, [4, NT * 128], BF16, [("kT", c, g) for c in range(4) for g in range(6)])
            dump("qT", qT, [4, QC], BF16, [("qT", c, g) for c in range(4) for g in range(1, 6)])
            dump("V1", V1, [NT, 8, 65], BF16, [("V1", w) for w in range(NT)] + ["V1ones"])
            dump("hglu", hglu, [4, QC + 30], BF16, [("hglu", c, g) for c in range(4) for g in range(1, 6)] + ["hglu_pad"])
            S.fence()

            dwd = view(R_U, [4, 31, 128], BF16)
            cf = view(R_T, [4, 512], F32)
            cb = view(R_T + 8192, [4, 512], BF16)
            csq = view(R_T + 12288, [4, 512], BF16)
            mean = view(R_T + 16384, [512], F32)
            var = view(R_T + 18432, [512], F32)
            rstdv = var
            zt = [view(R_T + 20480 + i * 2048, [512], F32) for i in range(2)]
            for c in (0, 2, 1, 3):
                if c < 2:
                    S.op("dve", lambda c=c: nc.vector.tensor_tensor(
                        out=dwd[:, c, :, :], in0=ident_f.unsqueeze(1).to_broadcast([128, 31, 128]),
                        in1=cw[:, c, :].unsqueeze(2).to_broadcast([128, 31, 128]), op=ALU.mult),
                         reads=["ident_f", "cw"], writes=[("dwd", c)])
                else:
                    S.op("pool", lambda c=c: nc.gpsimd.tensor_tensor(
                        out=dwd[:, c, :, :], in0=ident_f.unsqueeze(1).to_broadcast([128, 31, 128]),
                        in1=cw[:, c, :].unsqueeze(2).to_broadcast([128, 31, 128]), op=ALU.mult),
                         reads=["ident_f", "cw"], writes=[("dwd", c)])
            E = view(R_T + 26624, [8, 640], BF16)
            etmp = [view(R_CAT + 17408 + i * 2560, [640], F32) for i in range(2)]
            maskT = view(R_CAT + 17408 + 5120, [640], F32)
            S.dma("sp", "maskT", lambda: nc.sync.dma_start(out=maskT, in_=maskT_d), writes=["maskT"])
            def build_E(h):
                et = etmp[h % 2]; ek = ("etmp", h % 2)
                S.dma("sp", "et%d" % (h % 2), lambda: nc.sync.dma_start(out=et, in_=biasT_d[h]), writes=[ek])
                S.op("dve", lambda: nc.vector.tensor_tensor(out=et, in0=et, in1=maskT, op=ALU.add), reads=[ek, "maskT"], writes=[ek])
                S.op("act", lambda: nc.scalar.activation(out=E[:, h, :], in_=et, func=AF.Exp), reads=[ek], writes=[("E", h)])
            e_pending = list(range(8))
            a3_hooks = []
            a2_order = [gi for gi, g in enumerate(groups) if g[2]]
            a2_order = a2_order[1:] + a2_order[:1]
            for gi in a2_order:
                t0, nt, full = groups[gi]
                N = nt * 128
                ct0 = (t0 - 4) * 128
                for c in range(4):
                    b = nbank()
                    for j in range(31):
                        S.mm(lambda j=j, b=b, c=c: nc.tensor.matmul(pf[b][:, :N], lhsT=dwd[:, c, j, :],
                                                                    rhs=hglu[:, c, ct0 + j:ct0 + j + N], start=(j == 0), stop=(j == 30)),
                             reads=[("dwd", c)] + [("hglu", c, g) for g in range(1, 6)] + ["hglu_pad"],
                             writes=[("pf", b)], last=(j == 30))
                    S.op("act", lambda b=b, c=c: nc.scalar.activation(out=cf[:, c, :N], in_=pf[b][:, :N], func=AF.Identity,
                                                                      bias=cvec[:, c, 0:1], scale=1.0),
                         reads=[("pf", b), "cvec"], writes=[("cf", c)])
                    S.op("act", lambda b=b, c=c: nc.scalar.activation(out=csq[:, c, :N], in_=pf[b][:, :N], func=AF.Square,
                                                                      bias=cvec[:, c, 0:1], scale=1.0),
                         reads=[("pf", b), "cvec"], writes=[("csq", c)])
                    S.op("dve", lambda c=c: nc.vector.tensor_copy(out=cb[:, c, :N], in_=cf[:, c, :N]),
                         reads=[("cf", c)], writes=[("cb", c)])
                    if e_pending:
                        build_E(e_pending.pop(0))
                b1 = nbank()
                for c in range(4):
                    S.mm(lambda c=c, b1=b1: nc.tensor.matmul(pf[b1][:, :N], lhsT=ones_b, rhs=cb[:, c, :N], start=(c == 0), stop=(c == 3)),
                         reads=["ones_b", ("cb", c)], writes=[("pf", b1)], last=(c == 3))
                b2 = nbank()
                for c in range(4):
                    S.mm(lambda c=c, b2=b2: nc.tensor.matmul(pf[b2][:, :N], lhsT=ones_b, rhs=csq[:, c, :N], start=(c == 0), stop=(c == 3)),
                         reads=["ones_b", ("csq", c)], writes=[("pf", b2)], last=(c == 3))
                S.op("act", lambda b1=b1: nc.scalar.mul(out=mean[:, :N], in_=pf[b1][:, :N], mul=1.0 / 512), reads=[("pf", b1)], writes=["mean"])
                S.op("dve", lambda: nc.vector.tensor_tensor(out=var[:, :N], in0=mean[:, :N], in1=mean[:, :N], op=ALU.mult),
                     reads=["mean"], writes=["var", "rstdv"])
                S.op("dve", lambda b2=b2: nc.vector.scalar_tensor_tensor(out=var[:, :N], in0=pf[b2][:, :N], scalar=1.0 / 512, in1=var[:, :N],
                                                                         op0=ALU.mult, op1=ALU.subtract),
                     reads=[("pf", b2), "var"], writes=["var"])
                S.op("dve", lambda: nc.vector.tensor_scalar(out=var[:, :N], in0=var[:, :N], scalar1=0.0, scalar2=EPS, op0=ALU.max, op1=ALU.add),
                     reads=["var"], writes=["var"])
                S.op("act", lambda: nc.scalar.activation(out=var[:, :N], in_=var[:, :N], func=AF.Ln), reads=["var"], writes=["var"])
                S.op("act", lambda: nc.scalar.activation(out=rstdv[:, :N], in_=var[:, :N], func=AF.Exp, scale=-0.5),
                     reads=["var"], writes=["var", "rstdv"])
                def norm_block(c, N=N, ct0=ct0, gi=gi):
                    z = zt[c % 2]; zk = ("zt", c % 2)
                    S.op("pool", lambda: nc.gpsimd.tensor_tensor(out=z[:, :N], in0=cf[:, c, :N], in1=mean[:, :N], op=ALU.subtract),
                         reads=[("cf", c), "mean"], writes=[zk])
                    S.op("dve", lambda: nc.vector.tensor_tensor(out=z[:, :N], in0=z[:, :N], in1=rstdv[:, :N], op=ALU.mult),
                         reads=[zk, "rstdv"], writes=[zk])
                    S.op("act", lambda: nc.scalar.activation(out=conv_outT[:, c, ct0:ct0 + N], in_=z[:, :N], func=AF.Silu,
                                                             scale=cvec[:, c, 1:2], bias=cvec[:, c, 2:3]),
                         reads=[zk, "cvec"], writes=[("coT", c, gi)])

                if gi == a2_order[-1]:
                    a3_hooks.append(lambda f=norm_block: [f(c) for c in range(4)])
                else:
                    for c in range(4):
                        norm_block(c)
            while e_pending:
                build_E(e_pending.pop(0))
            dump("coT", conv_outT, [4, QC], BF16, [("coT", c, g) for c in range(4) for g in range(1, 6)])

            pT_ring = [view(5120 + i * 2048, [2, 512], BF16) for i in range(3)]
            attn_tok = view(11264, [4, 512], BF16)
            rden = view(15360, [8], F32)
            W_out = view(R_T + 8192, [8, 1024], BF16)
            w_out_v = w_out_d.rearrange("(k p) c -> p k c", p=128)
            a2_keys = [("cf", c) for c in range(4)] + [("cb", c) for c in range(4)] + [("csq", c) for c in range(4)] + \
                      ["mean", "var", "rstdv", ("zt", 0), ("zt", 1)]
            wout_loaded = [False]

            def load_wout():
                if wout_loaded[0]:
                    return
                wout_loaded[0] = True
                for k in range(8):
                    S.dma("pool", "w_out%d" % k, lambda k=k: nc.gpsimd.dma_start(out=W_out[:, k, :], in_=w_out_v[:, k, :]),
                          writes=[("W_out", k)] + a2_keys)
            obank = [ptb[i][:, :].bitcast(F32) for i in range(2)]
            LA = 2
            QL = 32
            S.op("pool", lambda: nc.gpsimd.memset(attn_outT[:, :, 0:128 - QL], 0.0), writes=[("aoT", 4, "pad")])
            for gi, (n0, nt, full) in enumerate(groups):
                if not full:
                    continue
                if n0 == 4 and nt == 1:
                    kts = [4, 3, 2, 1, 0]
                    for hp in range(4):
                        si = hp
                        pp = PP[si % 3]
                        ppk = [("pf", 2 * (si % 3)), ("pf", 2 * (si % 3) + 1)]
                        pT = pT_ring[si % 3]; pk = ("pT", si % 3)
                        for hh in range(2):
                            p0 = hh * 64
                            for j, kt in enumerate(kts):
                                S.mm(lambda hh=hh, p0=p0, j=j, kt=kt: nc.tensor.matmul(
                                    pp[:, hh, j * QL:(j + 1) * QL], lhsT=kT[p0:p0 + 64, hp, kt * 128:(kt + 1) * 128],
                                    rhs=qT[p0:p0 + 64, hp, 128 - QL:128], start=(j == 0), stop=(j == 4), skip_group_check=True),
                                     reads=[("kT", hp, g) for g in range(6)] + [("qT", hp, g) for g in range(1, 6)],
                                     writes=ppk, last=(hh == 1 and j == 4))
                        S.op("act", lambda: nc.scalar.activation(out=pT[:, :, 0:5 * QL], in_=pp[:, :, 0:5 * QL], func=AF.Exp),
                             reads=ppk, writes=[pk])
                        ev = E[:, 2 * hp:2 * hp + 2, :].rearrange("p h (j x) -> p h j x", x=128)[:, :, :, 128 - QL:128]
                        S.op("dve", lambda: nc.vector.tensor_tensor(
                            out=pT[:, :, 0:5 * QL].rearrange("p h (j x) -> p h j x", x=QL),
                            in0=pT[:, :, 0:5 * QL].rearrange("p h (j x) -> p h j x", x=QL), in1=ev, op=ALU.mult),
                             reads=[pk, ("E", 2 * hp), ("E", 2 * hp + 1)], writes=[pk])
                        for hh in range(2):
                            h = 2 * hp + hh
                            for j, kt in enumerate(kts):
                                S.mm(lambda hh=hh, h=h, j=j, kt=kt: nc.tensor.matmul(
                                    obank[hh][0:QL, 0:65], lhsT=pT[:, hh, j * QL:(j + 1) * QL], rhs=V1[:, kt, h, :],
                                    start=(j == 0), stop=(j == 4), skip_group_check=True),
                                     reads=[pk, ("V1", kt), "V1ones"], writes=[("ptb", hh)], last=(j == 4))
                            rd = rden[0:QL, hh * 4:hh * 4 + 1]
                            rk = ("rden", hh)
                            S.op("dve", lambda hh=hh, rd=rd: nc.vector.reciprocal(out=rd, in_=obank[hh][0:QL, 64:65]),
                                 reads=[("ptb", hh)], writes=[rk])
                            S.op("dve", lambda hh=hh, rd=rd, h=h: nc.vector.tensor_tensor(
                                out=attn_tok[0:QL, 0, h * 64:(h + 1) * 64], in0=obank[hh][0:QL, 0:64],
                                in1=rd.to_broadcast([QL, 64]), op=ALU.mult),
                                 reads=[("ptb", hh), rk], writes=[("atok", h)])
                    pt = ptb[0]; ptk = ("ptb", 0)
                    for c in range(4):
                        S.mm(lambda c=c: nc.tensor.transpose(out=pt[:, c * QL:(c + 1) * QL], in_=attn_tok[0:QL, 0, c * 128:(c + 1) * 128],
                                                             identity=ident_b[0:QL, 0:QL]),
                             reads=[("atok", h_) for h_ in range(8)] + ["ident_b"], writes=[ptk], last=(c == 3))
                    S.op("act", lambda: nc.scalar.copy(out=attn_outT[:, :, 128 - QL:128],
                                                       in_=pt[:, 0:4 * QL].rearrange("p (c t) -> p c t", c=4)),
                         reads=[ptk, ("aoT", 4, "pad")], writes=[("aoT", 4)])
                    continue
                n1 = n0 + nt - 1
                steps = [(hp, kt) for hp in range(4) for kt in range(n0 - 4, n1 + 1)]
                info = {}

                def emit_score(si):
                    hp, kt = steps[si]
                    a, bq = max(kt, n0), min(kt + 4, n1)
                    Nq = (bq - a + 1) * 128
                    pp = PP[si % 3]
                    for hh in range(2):
                        p0 = hh * 64
                        S.mm(lambda hh=hh, p0=p0: nc.tensor.matmul(pp[:, hh, :Nq], lhsT=kT[p0:p0 + 64, hp, kt * 128:(kt + 1) * 128],
                                                                   rhs=qT[p0:p0 + 64, hp, (a - 4) * 128:(bq - 3) * 128],
                                                                   start=True, stop=True),
                             reads=[("kT", hp, g) for g in range(6)] + [("qT", hp, g) for g in range(1, 6)],
                             writes=[("pf", 2 * (si % 3)), ("pf", 2 * (si % 3) + 1)], last=(hh == 1))
                    info[si] = (a, bq, Nq)

                for si in range(min(LA, len(steps))):
                    emit_score(si)
                for si, (hp, kt) in enumerate(steps):
                    if si + LA < len(steps):
                        emit_score(si + LA)
                    a, bq, Nq = info[si]
                    if a3_hooks and si >= 2:
                        a3_hooks.pop(0)()
                    elif not a3_hooks:
                        load_wout()
                    pp = PP[si % 3]
                    ppk = [("pf", 2 * (si % 3)), ("pf", 2 * (si % 3) + 1)]
                    pT = pT_ring[si % 3]; pk = ("pT", si % 3)
                    S.op("act", lambda: nc.scalar.activation(out=pT[:, :, :Nq], in_=pp[:, :, :Nq], func=AF.Exp),
                         reads=ppk, writes=[pk])
                    S.op("dve", lambda: nc.vector.tensor_tensor(out=pT[:, :, :Nq], in0=pT[:, :, :Nq],
                                                                in1=E[:, 2 * hp:2 * hp + 2, (a - kt) * 128:(bq - kt + 1) * 128], op=ALU.mult),
                         reads=[pk, ("E", 2 * hp), ("E", 2 * hp + 1)], writes=[pk])
                    for hh in range(2):
                        h = 2 * hp + hh
                        for n in range(a, bq + 1):
                            first = (kt == n0 - 4 and n == a)
                            lastmm = (kt == n1 and n == bq)
                            S.mm(lambda n=n, first=first, lastmm=lastmm, hh=hh, h=h: nc.tensor.matmul(
                                obank[hh][:, (n - n0) * 65:(n - n0) * 65 + 65], lhsT=pT[:, hh, (n - a) * 128:(n - a + 1) * 128],
                                rhs=V1[:, kt, h, :], start=first, stop=lastmm, skip_group_check=True),
                                 reads=[pk, ("V1", kt), "V1ones"], writes=[("ptb", hh)], last=(n == bq))
                    if kt == n1:
                        ov2 = ptb2[:, :, :].bitcast(F32)[:, :, 0:nt * 65].rearrange("p h (n d) -> p h n d", n=nt)
                        rd2 = rden[:, 0:8].rearrange("p (h n) -> p h n", h=2)[:, :, 0:nt]
                        S.op("dve", lambda: nc.vector.reciprocal(out=rd2.unsqueeze(3), in_=ov2[:, :, :, 64:65]),
                             reads=[("ptb", 0), ("ptb", 1)], writes=["rden2"])
                        S.op("dve", lambda: nc.vector.tensor_tensor(
                            out=attn_tok[:, 0:nt, 2 * hp * 64:(2 * hp + 2) * 64].rearrange("p n (h d) -> p h n d", h=2),
                            in0=ov2[:, :, :, 0:64], in1=rd2.unsqueeze(3).to_broadcast([128, 2, nt, 64]), op=ALU.mult),
                             reads=[("ptb", 0), ("ptb", 1), "rden2"], writes=[("atok", 2 * hp), ("atok", 2 * hp + 1)])
                for n in range(n0, n1 + 1):
                    pt = ptb[n % 2]; ptk = ("ptb", n % 2)
                    for c in range(4):
                        S.mm(lambda c=c, n=n, pt=pt: nc.tensor.transpose(out=pt[:, c * 128:(c + 1) * 128],
                                                                         in_=attn_tok[:, n - n0, c * 128:(c + 1) * 128], identity=ident_b),
                             reads=[("atok", h_) for h_ in range(8)] + ["ident_b"], writes=[ptk], last=(c == 3))
                    S.op("dve", lambda n=n, pt=pt: nc.vector.tensor_copy(out=attn_outT[:, :, (n - 4) * 128:(n - 3) * 128],
                                                                         in_=pt[:, 0:512].rearrange("p (c t) -> p c t", c=4)),
                         reads=[ptk], writes=[("aoT", n)])
            dump("aoT", attn_outT, [4, QC], BF16, [("aoT", n) for n in range(4, NT)])
            S.fence()

            for i in range(1, 4):
                S.dma("act", "c_g%d" % i, lambda i=i: nc.scalar.dma_start(out=gb[i], in_=gvec_d[i:i + 1, :].partition_broadcast(128)),
                      writes=[("gb", i)])
            xt_ring = [view(R_KQV + 16384 + i * 4096, [1024], F32) for i in range(3)]
            h1_ring = [view(R_KQV + 28672 + i * 4096, [1024], F32) for i in range(4)]
            ub_ring = [view(R_KQV + 45056 + i * 2048, [1024], BF16) for i in range(2)]
            junk = view(R_KQV + 49152, [1024], BF16)
            S.op("pool", lambda: nc.gpsimd.memset(u2T[:, :, 2050:2052], 0.0), writes=["u2T_pad"])
            O_WD = R_CAT
            O_ACT = O_WD + 45056
            O_WR = O_ACT + 45232
            NRING = 8
            PF = 6
            wring = [view(O_WR + i * 2048, [8, 128], BF16) for i in range(NRING)]
            w_up_v = w_up_d.rearrange("(k p) c -> p k c", p=128)
            seq = []
            for H in range(2):
                for j in range(22):
                    seq.append((H, j, 0))
                    seq.append((H, j, 1))

            def load_wup(si):
                H, j, isval = seq[si]
                cbk = (22 + j) if isval else j
                slot = si % NRING
                S.dma("pool", "wup%d" % slot, lambda: nc.gpsimd.dma_start(out=wring[slot], in_=w_up_v[:, :, cbk * 128:(cbk + 1) * 128]),
                      writes=[("wring", slot)])

            for si in range(PF):
                load_wup(si)

            def a4_s0(i):
                w = i + 4
                xt = xt_ring[i % 3]
                S.dma("sp", "xt4%d" % (i % 3), lambda: nc.sync.dma_start(out=xt, in_=xw[w * 128:(w + 1) * 128, :]),
                      writes=[("xt4", i % 3)])
                pp = PP[i % 3]
                for half in range(2):
                    for k in range(8):
                        src = conv_outT if k < 4 else attn_outT
                        S.mm(lambda k=k, half=half, src=src: nc.tensor.matmul(
                            pp[:, half, :], lhsT=src[:, k % 4, i * 128:(i + 1) * 128],
                            rhs=W_out[:, k, half * 512:(half + 1) * 512], start=(k == 0), stop=(k == 7)),
                             reads=[("W_out", k)] + ([("coT", k, g) for g in range(1, 6)] if k < 4 else [("aoT", w), ("aoT", 4, "pad")]),
                             writes=[("pf", 2 * (i % 3) + half)], last=(k == 7))

            def a4_s1(i):
                pp = PP[i % 3]
                xt = xt_ring[i % 3]; xk = ("xt4", i % 3)
                h1 = h1_ring[i % 4]; hk = ("h1", i % 4)
                c_sum, c_r = newstat(), newstat()
                S.op("act", lambda: nc.scalar.activation(out=junk.rearrange("p (a b) -> p a b", a=2), in_=pp[:, :, :], func=AF.Square,
                                                         accum_out=stat[:, c_sum:c_sum + 1]),
                     reads=[("pf", 2 * (i % 3)), ("pf", 2 * (i % 3) + 1)], writes=["junk4", ("st", c_sum)])
                rstd_act(c_sum, c_r, D, [("st", c_sum)])
                for half in range(2):
                    S.op("dve", lambda half=half: nc.vector.scalar_tensor_tensor(
                        out=h1[:, half * 512:(half + 1) * 512], in0=pp[:, half, :], scalar=stat[:, c_r:c_r + 1],
                        in1=gb[1][:, half * 512:(half + 1) * 512], op0=ALU.mult, op1=ALU.mult),
                         reads=[("pf", 2 * (i % 3) + half), ("st", c_r), ("gb", 1)], writes=[hk])
                S.op("pool", lambda: nc.gpsimd.tensor_tensor(out=h1, in0=h1, in1=xt, op=ALU.add),
                     reads=[hk, xk], writes=[hk])

            def a4_s2(i):
                h1 = h1_ring[i % 4]; hk = ("h1", i % 4)
                ub = ub_ring[i % 2]; ubk = ("ub4", i % 2)
                if i >= 1:
                    S.dma("sp", "h1w%d" % (i % 4), lambda: nc.sync.dma_start(out=h1s[(i - 1) * 128:i * 128, :], in_=h1),
                          reads=[hk], writes=[("h1s", i - 1)])
                c2, c2r = newstat(), newstat()
                S.op("act", lambda: nc.scalar.activation(out=junk, in_=h1, func=AF.Square, accum_out=stat[:, c2:c2 + 1]),
                     reads=[hk], writes=["junk4", ("st", c2)])
                rstd_act(c2, c2r, D, [("st", c2)])
                S.op("dve", lambda: nc.vector.scalar_tensor_tensor(out=ub, in0=h1, scalar=stat[:, c2r:c2r + 1], in1=gb[2],
                                                                   op0=ALU.mult, op1=ALU.mult),
                     reads=[hk, ("st", c2r), ("gb", 2)], writes=[ubk])

            def a4_s2b(i):
                ub = ub_ring[i % 2]; ubk = ("ub4", i % 2)
                pt = ptb[i % 2]; ptk = ("ptb", i % 2)
                for k in range(8):
                    S.mm(lambda k=k: nc.tensor.transpose(out=pt[:, k * 128:(k + 1) * 128], in_=ub[:, k * 128:(k + 1) * 128],
                                                         identity=ident_b),
                         reads=[ubk, "ident_b"], writes=[ptk], last=(k == 7))

            def a4_s3(i):
                pt = ptb[i % 2]; ptk = ("ptb", i % 2)
                ptv = pt[:, :].rearrange("p (k t) -> p k t", k=8)
                if i >= 1:
                    S.op("act", lambda: nc.scalar.copy(out=u2T[:, :, 2 + (i - 1) * 128:2 + i * 128], in_=ptv),
                         reads=[ptk], writes=[("u2T", i)])
                else:
                    S.op("act", lambda: nc.scalar.activation(out=u2T[:, :, 0:2], in_=ptv[:, :, 126:128], func=AF.Identity,
                                                             scale=flag[:, 0:1]),
                         reads=[ptk, "flag"], writes=[("u2T", 0)])

            for t in range(NQ + 5):
                if 0 <= t - 5 < NQ:
                    a4_s3(t - 5)
                if 0 <= t - 4 < NQ:
                    a4_s2b(t - 4)
                if 0 <= t - 3 < NQ:
                    a4_s2(t - 3)
                if 0 <= t - 1 < NQ:
                    a4_s1(t - 1)
                if t < NQ:
                    a4_s0(t)
            dump("u2T", u2T, [8, 2052], BF16, [("u2T", i) for i in range(NQ)] + ["u2T_pad"])
            S.fence()

        with ExitStack() as psB:
            PU = [psB.enter_context(nc.psum_tensor("pu%d" % i, [128, 3, 512], F32)) for i in range(2)]
            PD = psB.enter_context(nc.psum_tensor("pd", [128, 1024], F32))
            o = R_CAT
            W_down = view(o, [22, 1024], BF16); o += 45056
            act = view(o, [22, 1028], BF16); o += 45232
            assert o == O_WR
            o += NRING * 2048
            tA = [view(o + i * 4104, [3, 342], F32) for i in range(3)]; o += 3 * 4104
            tB = [view(o, [3, 342], F32)]; o += 4104
            h1r = [view(o + i * 4096, [1024], F32) for i in range(2)]; o += 8192
            ot = [view(o + i * 4096, [1024], F32) for i in range(2)]; o += 8192
            junk = view(o, [1024], BF16); o += 2048
            xsb = [view(o + i * 4128, [3, 344], F32) for i in range(2)]; o += 2 * 4128
            assert o <= 204800, o
            w_down_v = w_down_d.rearrange("(f p) c -> p f c", p=128)

            def emit_wdown(H):
                b0 = 1024 * H
                for i in range(8):
                    tile = H * 8 + i
                    if i % 2 == 0:
                        pdv, pdk = PD[:, :], "PD"
                    else:
                        pdv, pdk = PU[1][:, 0:2, :].rearrange("p a b -> p (a b)"), ("PU", 1)
                    for half in range(2):
                        for f in range(22):
                            S.mm(lambda f=f, half=half, i=i: nc.tensor.matmul(
                                pdv[:, half * 512:(half + 1) * 512], lhsT=act[:, f, i * 128:(i + 1) * 128],
                                rhs=W_down[:, f, half * 512:(half + 1) * 512], start=(f == 0), stop=(f == 21)),
                                 reads=[("act", f), ("W_down", f)], writes=[pdk], last=(f == 21))
                    hr = h1r[tile % 2]; hrk = ("h1r", tile % 2)
                    S.dma("sp", "h1r%d" % (tile % 2), lambda tile=tile, hr=hr: nc.sync.dma_start(out=hr, in_=h1s[tile * 128:(tile + 1) * 128, :]),
                          reads=[("h1s", tile)], writes=[hrk])
                    c_s, c_r = newstat(), newstat()
                    S.op("act", lambda c_s=c_s: nc.scalar.activation(out=junk, in_=pdv, func=AF.Square, accum_out=stat[:, c_s:c_s + 1]),
                         reads=[pdk], writes=["junkB", ("st", c_s)])
                    rstd_from(c_s, c_r, D, [("st", c_s)])
                    oo = ot[tile % 2]; ok = ("ot", tile % 2)
                    S.op("dve", lambda oo=oo, c_r=c_r: nc.vector.scalar_tensor_tensor(out=oo, in0=pdv, scalar=stat[:, c_r:c_r + 1], in1=gb[3],
                                                                                     op0=ALU.mult, op1=ALU.mult),
                         reads=[pdk, ("st", c_r), ("gb", 3)], writes=[ok])
                    S.op("pool", lambda oo=oo, hr=hr: nc.gpsimd.tensor_tensor(out=oo, in0=oo, in1=hr, op=ALU.add),
                         reads=[ok, hrk], writes=[ok])
                    S.dma("sp", "ow%d" % (tile % 2), lambda tile=tile, oo=oo: nc.sync.dma_start(out=out_d[tile * 128:(tile + 1) * 128, :], in_=oo),
                          reads=[ok], writes=[("out", tile)])

            pending_tail = [None]
            for si, (H, j, isval) in enumerate(seq):
                if si + PF < len(seq):
                    load_wup(si + PF)
                if si < 22:
                    S.dma("pool", "w_down%d" % si, lambda f=si: nc.gpsimd.dma_start(out=W_down[:, f, :], in_=w_down_v[:, f, :]),
                          writes=[("W_down", si)])
                b0 = 1024 * H
                cbk = (22 + j) if isval else j
                slot = si % NRING
                pu = PU[si % 2]; puk = ("PU", si % 2)
                for k in range(8):
                    for v in range(3):
                        S.mm(lambda k=k, v=v, pu=pu: nc.tensor.matmul(pu[:, v, 0:344], lhsT=wring[slot][:, k, :],
                                                                      rhs=u2T[:, k, b0 + 342 * v:b0 + 342 * v + 344],
                                                                      start=(k == 0), stop=(k == 7)),
                             reads=[("wring", slot)] + [("u2T", i_) for i_ in range(NQ)] + ["u2T_pad"],
                             writes=[puk], last=(k == 7 and v == 2))
                ta = tA[si % 3]; tak = ("tA", si % 3)
                tb = tB[0]; tbk = ("tB", 0)
                xs = xsb[si % 2]; xsk = ("xs", si % 2)
                S.op("act", lambda: nc.scalar.copy(out=xs, in_=pu[:, :, 0:344]), reads=[puk], writes=[xsk])
                S.op("act", lambda: nc.scalar.activation(out=ta, in_=pu[:, :, 2:344], func=AF.Identity,
                                                         scale=fdw[:, cbk, 2:3], bias=fdw[:, cbk, 3:4]),
                     reads=[puk, "fdw"], writes=[tak])
                S.op("dve", lambda: nc.vector.scalar_tensor_tensor(out=tb, in0=xs[:, :, 1:343], scalar=fdw[:, cbk, 1:2],
                                                                   in1=ta, op0=ALU.mult, op1=ALU.add),
                     reads=[xsk, "fdw", tak], writes=[tbk])
                S.op("dve", lambda: nc.vector.scalar_tensor_tensor(out=ta, in0=xs[:, :, 0:342], scalar=fdw[:, cbk, 0:1],
                                                                   in1=tb, op0=ALU.mult, op1=ALU.add),
                     reads=[xsk, "fdw", tbk], writes=[tak])
                av = act[:, j, 0:1026].rearrange("p (v m) -> p v m", v=3)
                if pending_tail[0] is not None:
                    pending_tail[0]()
                if not isval:
                    def tail(av=av, ta=ta, tak=tak, j=j):
                        S.op("act", lambda: nc.scalar.activation(out=av, in_=ta, func=AF.Gelu_apprx_tanh),
                             reads=[tak], writes=[("act", j)])
                else:
                    def tail(av=av, ta=ta, tak=tak, j=j):
                        S.op("dve", lambda: nc.vector.tensor_tensor(out=av, in0=av, in1=ta, op=ALU.mult),
                             reads=[tak, ("act", j)], writes=[("act", j)])
                pending_tail[0] = tail
                if j == 21 and isval:
                    pending_tail[0]()
                    pending_tail[0] = None
                    emit_wdown(H)

            for slot, ds in S.dsem.items():
                if slot.startswith("ow") or slot.startswith("dbg_"):
                    nc.sync.wait_ge(ds[0], ds[1])
        build_program.ninst = dict(S.ninst)
    return nc, dbg


_CACHE = {}


def _prep_shared(inp):
    f32 = np.float32
    rel = np.asarray(inp["rel_bias"], f32)[0]
    ki = np.arange(128)[:, None]
    qi = np.arange(640)[None, :]
    idx = np.clip(qi - ki, -128, 128) + 128
    biasT = np.ascontiguousarray(rel[:, idx])
    dc = qi // 64 - ki // 64
    maskT = np.where((dc >= 0) & (dc <= 8), 0.0, -30000.0).astype(f32)
    cw = np.ascontiguousarray(np.asarray(inp["conv_dw_w"], f32)[0].T.reshape(4, 128, 31).transpose(1, 0, 2)).reshape(128, 124)
    cv = np.stack([np.asarray(inp["conv_dw_b"], f32)[0], np.asarray(inp["conv_ln_g"], f32)[0],
                   np.asarray(inp["conv_ln_b"], f32)[0]])
    cvec = np.ascontiguousarray(cv.reshape(3, 4, 128).transpose(2, 1, 0)).reshape(128, 12)
    fd = np.concatenate([np.asarray(inp["ffn_dw_w"], f32)[0], np.asarray(inp["ffn_dw_b"], f32)], axis=0)
    fdw = np.ascontiguousarray(fd.reshape(4, 44, 128).transpose(2, 1, 0)).reshape(128, 176)
    gvec = np.ascontiguousarray(np.concatenate([np.asarray(inp[k], f32) for k in
                                                ("norm_mix_pre", "norm_mix_post", "norm_ffn_pre", "norm_ffn_post")], axis=0))
    return {
        "w_in": np.ascontiguousarray(np.asarray(inp["w_in"], f32)[0]),
        "w_out": np.ascontiguousarray(np.asarray(inp["w_out"], f32)[0]),
        "w_up": np.ascontiguousarray(np.asarray(inp["w_up"], f32)[0]),
        "w_down": np.ascontiguousarray(np.asarray(inp["w_down"], f32)[0]),
        "gvec": gvec, "cw": cw, "cvec": cvec, "fdw": fdw, "biasT": biasT, "maskT": maskT,
        "ident": np.eye(128, dtype=f32),
    }


def kernel(**inputs):
    x = np.asarray(inputs["x"], np.float32)
    shared = _prep_shared(inputs)
    in_maps = []
    for c in range(8):
        b, s = c // 4, (c % 4) * OWN
        xwin = np.zeros((NT * 128, D), np.float32)
        lo = s - 640
        src_lo = max(lo, 0)
        xwin[src_lo - lo:, :] = x[b, src_lo:s + OWN, :]
        m = dict(shared)
        m["xw"] = xwin
        m["flag"] = np.full((128, 1), 1.0 if s > 0 else 0.0, np.float32)
        in_maps.append(m)
    if "nc" not in _CACHE:
        _CACHE["nc"] = build_program()
    nc, dbg = _CACHE["nc"]
    res = run_bass_kernel_spmd(nc, in_maps, core_ids=list(range(8)))
    kernel.last_results = res
    out = np.empty((2, 8192, D), np.float32)
    for c in range(8):
        b, s = c // 4, (c % 4) * OWN
        out[b, s:s + OWN, :] = res.results[c]["out"]
    return out
```

```python
import numpy as np
from contextlib import ExitStack
import concourse.bass as bass
import concourse.mybir as mybir
from concourse.bass_utils import run_bass_kernel_spmd

F32 = mybir.dt.float32
BF16 = mybir.dt.bfloat16
AF = mybir.ActivationFunctionType
ALU = mybir.AluOpType

D = 1024
NT = 21
NQ = 17
QC = NQ * 128
OWN = 2048
DFF = 2816
EPS = 1e-6
DEBUG = False


class Sched:
    def __init__(self, nc, es):
        self.nc = nc
        self.es = es
        self.eng = {"pe": nc.tensor, "act": nc.scalar, "dve": nc.vector,
                    "pool": nc.gpsimd, "sp": nc.sync}
        self.sem = {e: es.enter_context(nc.semaphore("s_" + e)) for e in self.eng}
        self.cnt = {e: 0 for e in self.eng}
        self.known = {e: {} for e in self.eng}
        self.res = {}
        self.dsem = {}
        self.ninst = {e: 0 for e in self.eng}

    def _need(self, e, tok, out):
        if tok is None:
            return
        sem, name, val, peng = tok
        if peng == "pe" and e == "pe":
            return
        if self.known[e].get(name, 0) >= val:
            return
        cur = out.get(name)
        if cur is None or cur[2] < val:
            out[name] = tok

    def _emit_waits(self, e, need):
        for name, tok in need.items():
            self.eng[e].wait_ge(tok[0], tok[2])
            self.known[e][name] = tok[2]

    def _deps(self, e, reads, writes):
        need = {}
        for r in reads:
            st = self.res.get(r)
            if st is not None:
                self._need(e, st["w"], need)
        for w in writes:
            st = self.res.get(w)
            if st is not None:
                self._need(e, st["w"], need)
                for t in st["r"].values():
                    self._need(e, t, need)
        self._emit_waits(e, need)

    def _record(self, tok, reads, writes):
        for r in reads:
            st = self.res.setdefault(r, {"w": None, "r": {}})
            cur = st["r"].get(tok[1])
            if cur is None or cur[2] < tok[2]:
                st["r"][tok[1]] = tok
        for w in writes:
            self.res[w] = {"w": tok, "r": {}}

    def op(self, e, fn, reads=(), writes=()):
        self._deps(e, reads, writes)
        ins = fn()
        self.cnt[e] += 1
        ins.then_inc(self.sem[e], 1)
        tok = (self.sem[e], "s_" + e, self.cnt[e], e)
        self._record(tok, reads, writes)
        self.ninst[e] += 1
        return tok

    def mm(self, fn, reads=(), writes=(), last=True):
        e = "pe"
        self._deps(e, reads, writes)
        ins = fn()
        self.ninst[e] += 1
        if last:
            self.cnt[e] += 1
            ins.then_inc(self.sem[e], 1)
            tok = (self.sem[e], "s_pe", self.cnt[e], e)
        else:
            tok = (self.sem[e], "s_pe", self.cnt[e] + 1, e)
        self._record(tok, reads, writes)
        return tok

    def dma(self, q, slot, fn, reads=(), writes=()):
        if slot not in self.dsem:
            self.dsem[slot] = [self.es.enter_context(self.nc.semaphore("d_" + slot)), 0]
        self._deps(q, reads, writes)
        inss = fn()
        if not isinstance(inss, (list, tuple)):
            inss = [inss]
        ds = self.dsem[slot]
        for ins in inss:
            ins.then_inc(ds[0], 16)
            ds[1] += 16
        tok = (ds[0], "d_" + slot, ds[1], None)
        self._record(tok, reads, writes)
        return tok

    def fence(self, engines=("pe", "act", "dve", "pool", "sp")):
        toks = []
        for f in self.eng:
            if self.cnt[f] > 0:
                toks.append((self.sem[f], "s_" + f, self.cnt[f], f))
        for slot, ds in self.dsem.items():
            if ds[1] > 0:
                toks.append((ds[0], "d_" + slot, ds[1], None))
        for e in engines:
            need = {}
            for t in toks:
                if t[3] == e:
                    continue
                self._need(e, t, need)
            self._emit_waits(e, need)


def build_program():
    nc = bass.Bass("TRN2", target_bir_lowering=False)
    dr = lambda name, shape, dt=F32, kind="ExternalInput": nc.dram_tensor(name, shape, dt, kind=kind).ap()
    xw = dr("xw", [NT * 128, D])
    flag_d = dr("flag", [128, 1])
    w_in_d = dr("w_in", [D, 2560])
    w_out_d = dr("w_out", [D, D])
    w_up_d = dr("w_up", [D, 2 * DFF])
    w_down_d = dr("w_down", [DFF, D])
    gvec_d = dr("gvec", [4, D])
    cw_d = dr("cw", [128, 4 * 31])
    cvec_d = dr("cvec", [128, 12])
    fdw_d = dr("fdw", [128, 44 * 4])
    biasT_d = dr("biasT", [8, 128, 640])
    maskT_d = dr("maskT", [128, 640])
    ident_d = dr("ident", [128, 128])
    out_d = dr("out", [OWN, D], F32, "ExternalOutput")
    h1s = nc.dram_tensor("h1s", [OWN, D], F32).ap()
    dbg = {}

    with ExitStack() as es:
        S = Sched(nc, es)
        big = es.enter_context(nc.sbuf_tensor("big", [128, 51200], F32))

        def view(off, shape, dt):
            n = int(np.prod(shape))
            esz = 4 if dt == F32 else 2
            nb = n * esz
            assert off % 4 == 0 and nb % 4 == 0 and off + nb <= 204800, (off, nb)
            ap = big[:, off // 4:(off + nb) // 4]
            if dt != F32:
                ap = ap.bitcast(dt)
            if len(shape) == 2:
                ap = ap.rearrange("p (a b) -> p a b", a=shape[0])
            elif len(shape) == 3:
                ap = ap.rearrange("p (a b c) -> p a b c", a=shape[0], b=shape[1])
            return ap

        def dump(name, ap, shape, dt, reads):
            if not DEBUG:
                return
            t = nc.dram_tensor("dbg_" + name, [128] + list(shape), dt, kind="ExternalOutput").ap()
            dbg[name] = t
            S.dma("sp", "dbg_" + name, lambda: nc.sync.dma_start(out=t, in_=ap), reads=reads)

        o = 0
        ident_f = view(o, [128], F32); o += 512
        ident_b = view(o, [128], BF16); o += 256
        ones_b = view(o, [128], BF16); o += 256
        gb = [view(o + i * 4096, [1024], F32) for i in range(4)]; o += 16384
        cw = view(o, [4, 31], F32); o += 496
        cvec = view(o, [4, 3], F32); o += 48
        fdw = view(o, [44, 4], F32); o += 704
        flag = view(o, [1], F32); o += 4
        stat = view(o, [448], F32); o += 1792
        nh1 = view(o, [1], F32); o += 4
        assert o <= 20480
        stat_i = [0]

        def newstat(n=1):
            i = stat_i[0]
            stat_i[0] += n
            assert stat_i[0] <= 448
            return i

        xt_ring = [view(166528 + i * 4096, [1024], F32) for i in range(4)]
        for w0 in range(4):
            S.dma("sp", "xt%d" % w0, lambda w0=w0: nc.sync.dma_start(out=xt_ring[w0], in_=xw[w0 * 128:(w0 + 1) * 128, :]),
                  writes=[("xt", w0)])
        S.dma("sp", "c_id", lambda: nc.sync.dma_start(out=ident_f, in_=ident_d), writes=["ident_f"])
        S.dma("sp", "c_flag", lambda: nc.sync.dma_start(out=flag, in_=flag_d), writes=["flag"])
        S.dma("act", "c_g0", lambda: nc.scalar.dma_start(out=gb[0], in_=gvec_d[0:1, :].partition_broadcast(128)),
              writes=[("gb", 0)])
        S.dma("sp", "c_cw", lambda: nc.sync.dma_start(out=cw, in_=cw_d.rearrange("p (a b) -> p a b", a=4)), writes=["cw"])
        S.dma("sp", "c_cvec", lambda: nc.sync.dma_start(out=cvec, in_=cvec_d.rearrange("p (a b) -> p a b", a=4)), writes=["cvec"])
        S.dma("sp", "c_fdw", lambda: nc.sync.dma_start(out=fdw, in_=fdw_d.rearrange("p (a b) -> p a b", a=44)), writes=["fdw"])
        S.op("dve", lambda: nc.vector.tensor_copy(out=ident_b, in_=ident_f), reads=["ident_f"], writes=["ident_b"])
        S.op("dve", lambda: nc.vector.memset(ones_b, 1.0), writes=["ones_b"])

        R_U = 20480
        R_CAT = 53312
        R_KQV = 88128
        R_HG = 148880
        R_T = 166528
        u2T = view(R_U, [8, 2052], BF16)
        conv_outT = view(R_CAT, [4, QC], BF16)
        attn_outT = view(R_CAT + 17408, [4, QC], BF16)
        kT = view(R_KQV, [4, NT * 128], BF16)
        qT = view(R_KQV + 21504, [4, QC], BF16)
        V1 = view(R_KQV + 21504 + 17408, [NT, 8, 65], BF16)
        hglu = view(R_HG, [4, QC + 30], BF16)

        def rstd_from(ssq_col, out_col, n_feat, reads):
            tmp = newstat()
            S.op("pool", lambda: nc.gpsimd.tensor_scalar(out=stat[:, tmp:tmp + 1], in0=stat[:, ssq_col:ssq_col + 1],
                                                        scalar1=1.0 / n_feat, scalar2=EPS, op0=ALU.mult, op1=ALU.add),
                 reads=reads, writes=[("st", tmp)])
            S.op("pool", lambda: nc.gpsimd.tensor_tensor(out=stat[:, out_col:out_col + 1], in0=stat[:, tmp:tmp + 1],
                                                        in1=nh1, op=ALU.pow),
                 reads=[("st", tmp), "nh"], writes=[("st", out_col)])

        def rstd_act(ssq_col, out_col, n_feat, reads):
            tmp = newstat()
            S.op("act", lambda: nc.scalar.activation(out=stat[:, tmp:tmp + 1], in_=stat[:, ssq_col:ssq_col + 1], func=AF.Ln,
                                                     scale=1.0 / n_feat, bias=EPS),
                 reads=reads, writes=[("st", tmp)])
            S.op("act", lambda: nc.scalar.activation(out=stat[:, out_col:out_col + 1], in_=stat[:, tmp:tmp + 1], func=AF.Exp,
                                                     scale=-0.5),
                 reads=[("st", tmp)], writes=[("st", out_col)])

        with ExitStack() as psA:
            PP = [psA.enter_context(nc.psum_tensor("pp%d" % i, [128, 2, 512], F32)) for i in range(3)]
            pf = [PP[i // 2][:, i % 2, :] for i in range(6)]
            ptb2 = psA.enter_context(nc.psum_tensor("ptb2", [128, 2, 1024], BF16))
            ptb = [ptb2[:, i, :] for i in range(2)]
            pfi = [0]

            def nbank():
                i = pfi[0] % 6
                pfi[0] += 1
                return i

            W_in = view(R_U, [8, 2560], BF16)
            uT_ring = [view(R_U + 40960 + i * 8192, [8, 512], BF16) for i in range(3)]
            ub_ring = [view(R_T + 16384 + i * 2048, [1024], BF16) for i in range(4)]
            sig_ring = [view(R_T + 24576 + i * 2048, [512], F32) for i in range(2)]
            junk = view(R_T + 28672, [1024], BF16)
            S.op("pool", lambda: nc.gpsimd.memset(nh1, -0.5), writes=["nh"])
            w_in_v = w_in_d.rearrange("(k p) c -> p k c", p=128)
            for k in range(8):
                S.dma("pool", "w_in%d" % k, lambda k=k: nc.gpsimd.dma_start(out=W_in[:, k, 1536:2560], in_=w_in_v[:, k, 1536:2560]),
                      writes=[("W_in", k, 0)])
            def a1_late_setup():
                for k in range(8):
                    S.dma("pool", "w_inb%d" % k, lambda k=k: nc.gpsimd.dma_start(out=W_in[:, k, 0:1536], in_=w_in_v[:, k, 0:1536]),
                          writes=[("W_in", k, 1)])
                S.op("pool", lambda: nc.gpsimd.memset(hglu[:, :, 0:30], 0.0), writes=["hglu_pad"])
                S.op("pool", lambda: nc.gpsimd.memset(V1[:, :, :, 64:65], 1.0), writes=["V1ones"])
                S.op("pool", lambda: nc.gpsimd.tensor_scalar(out=V1[:, 0:5, :, 64:65], in0=V1[:, 0:5, :, 64:65],
                                                            scalar1=flag[:, 0:1], scalar2=1e-30, op0=ALU.mult, op1=ALU.max),
                     reads=["flag", "V1ones"], writes=["V1ones"])


            groups = [(0, 4, False), (4, 1, True), (5, 4, True), (9, 4, True), (13, 4, True), (17, 4, True)]
            evac_flip = [0]
            tile_info = []
            for gi, (t0, nt, full) in enumerate(groups):
                for j in range(nt):
                    tile_info.append((gi, j, t0 + j, j == nt - 1))

            def a1_dma(ti):
                gi, j, w, lastj = tile_info[ti]
                if w < 4:
                    return
                xt = xt_ring[w % 4]
                S.dma("sp", "xt%d" % (w % 4), lambda: nc.sync.dma_start(out=xt, in_=xw[w * 128:(w + 1) * 128, :]),
                      writes=[("xt", w % 4)])

            def a1_s1a(ti):
                gi, j, w, lastj = tile_info[ti]
                xt = xt_ring[w % 4]; xk = ("xt", w % 4)
                ub = ub_ring[w % 4]; ubk = ("ub", w % 4)
                c_ssq, c_r = newstat(), newstat()
                S.op("act", lambda: nc.scalar.activation(out=junk, in_=xt, func=AF.Square, accum_out=stat[:, c_ssq:c_ssq + 1]),
                     reads=[xk], writes=["junk", ("st", c_ssq)])
                rstd_from(c_ssq, c_r, D, [("st", c_ssq)])
                S.op("dve", lambda: nc.vector.scalar_tensor_tensor(out=ub, in0=xt, scalar=stat[:, c_r:c_r + 1], in1=gb[0],
                                                                   op0=ALU.mult, op1=ALU.mult),
                     reads=[xk, ("st", c_r), ("gb", 0)], writes=[ubk])

            def a1_s1b(ti):
                gi, j, w, lastj = tile_info[ti]
                ub = ub_ring[w % 4]; ubk = ("ub", w % 4)
                pt = ptb[w % 2]; ptk = ("ptb", w % 2)
                for k in range(8):
                    S.mm(lambda k=k: nc.tensor.transpose(out=pt[:, k * 128:(k + 1) * 128], in_=ub[:, k * 128:(k + 1) * 128],
                                                         identity=ident_b),
                         reads=[ubk, "ident_b"], writes=[ptk], last=(k == 7))

            def a1_s2(ti):
                gi, j, w, lastj = tile_info[ti]
                ut = uT_ring[gi % 3]
                pt = ptb[w % 2]; ptk = ("ptb", w % 2)
                S.op("act", lambda: nc.scalar.copy(out=ut[:, :, j * 128:(j + 1) * 128],
                                                   in_=pt[:, :].rearrange("p (k t) -> p k t", k=8)),
                     reads=[ptk], writes=[("uT", gi % 3)])

            def a1_proj(gi, hooks):
                t0, nt, full = groups[gi]
                N = nt * 128
                ut = uT_ring[gi % 3]
                utk = ("uT", gi % 3)
                ct0 = (t0 - 4) * 128

                nblk = [0]

                def maybe_hook():
                    nblk[0] += 1
                    if hooks and nblk[0] > 2:
                        hooks.pop(0)()

                def proj_block(col0):
                    maybe_hook()
                    b = nbank()
                    for k in range(8):
                        S.mm(lambda k=k: nc.tensor.matmul(pf[b][:, :N], lhsT=W_in[:, k, col0:col0 + 128],
                                                          rhs=ut[:, k, :N], start=(k == 0), stop=(k == 7)),
                             reads=[("W_in", k, 0 if col0 >= 1536 else 1), utk], writes=[("pf", b)], last=(k == 7))
                    return b

                if full:
                    for c in range(4):
                        bg = proj_block((4 + c) * 128)
                        sg = sig_ring[c % 2]; sgk = ("sig", c % 2)
                        S.op("act", lambda: nc.scalar.activation(out=sg[:, :N], in_=pf[bg][:, :N], func=AF.Sigmoid),
                             reads=[("pf", bg)], writes=[sgk])
                        bv = proj_block(c * 128)
                        S.op("dve", lambda: nc.vector.tensor_tensor(out=hglu[:, c, 30 + ct0:30 + ct0 + N], in0=pf[bv][:, :N],
                                                                    in1=sg[:, :N], op=ALU.mult),
                             reads=[("pf", bv), sgk], writes=[("hglu", c, gi)])
                    for c in range(4):
                        b = proj_block((8 + c) * 128)
                        S.op("act", lambda: nc.scalar.mul(out=qT[:, c, ct0:ct0 + N], in_=pf[b][:, :N], mul=0.125),
                             reads=[("pf", b)], writes=[("qT", c, gi)])
                for c in range(4):
                    b = proj_block((12 + c) * 128)
                    S.op("dve", lambda: nc.vector.tensor_copy(out=kT[:, c, t0 * 128:t0 * 128 + N], in_=pf[b][:, :N]),
                         reads=[("pf", b)], writes=[("kT", c, gi)])
                for j in range(nt):
                    w = t0 + j
                    maybe_hook()
                    b = nbank()
                    for k in range(8):
                        S.mm(lambda k=k: nc.tensor.matmul(pf[b][:, :512], lhsT=ut[:, k, j * 128:(j + 1) * 128],
                                                          rhs=W_in[:, k, 2048:2560], start=(k == 0), stop=(k == 7)),
                             reads=[("W_in", k, 0), utk], writes=[("pf", b)], last=(k == 7))
                    src = pf[b][:, :512].rearrange("p (h d) -> p h d", h=8)
                    if evac_flip[0] % 2 == 0:
                        S.op("act", lambda: nc.scalar.copy(out=V1[:, w, :, 0:64], in_=src), reads=[("pf", b)], writes=[("V1", w)])
                    else:
                        S.op("dve", lambda: nc.vector.tensor_copy(out=V1[:, w, :, 0:64], in_=src), reads=[("pf", b)], writes=[("V1", w)])
                    evac_flip[0] += 1

            tiles_of = {}
            for ti, (gi, j, w, lastj) in enumerate(tile_info):
                tiles_of.setdefault(gi, []).append(ti)
            ng = len(groups)
            for ti in tiles_of[0]:
                a1_dma(ti)
            for ti in tiles_of[0]:
                a1_s1a(ti)
            a1_late_setup()
            for ti in tiles_of[1]:
                a1_dma(ti)
            for ti in tiles_of[0]:
                a1_s1b(ti)
                a1_s2(ti)
            for ti in tiles_of[1]:
                a1_s1a(ti)
            for g in range(ng):
                hooks = []
                if g + 1 < ng:
                    for ti in tiles_of[g + 1]:
                        hooks.append(lambda ti=ti: a1_s1b(ti))
                        hooks.append(lambda ti=ti: a1_s2(ti))
                if g + 2 < ng:
                    for ti in tiles_of[g + 2]:
                        a1_dma(ti)
                a1_proj(g, hooks)
                while hooks:
                    hooks.pop(0)()
                if g + 2 < ng:
                    for ti in tiles_of[g + 2]:
                        a1_s1a(ti)
            dump("kT", kT, [4, NT * 128], BF16, [("kT", c, g) for c in range(4) for g in range(6)])
            dump("qT", qT, [4, QC], BF16, [("qT", c, g) for c in range(4) for g in range(1, 6)])
            dump("V1", V1, [NT, 8, 65], BF16, [("V1", w) for w in range(NT)] + ["V1ones"])
            dump("hglu", hglu, [4, QC + 30], BF16, [("hglu", c, g) for c in range(4) for g in range(1, 6)] + ["hglu_pad"])
            S.fence()

            dwd = view(R_U, [4, 31, 128], BF16)
            cf = view(R_T, [4, 512], F32)
            cb = view(R_T + 8192, [4, 512], BF16)
            csq = view(R_T + 12288, [4, 512], BF16)
            mean = view(R_T + 16384, [512], F32)
            var = view(R_T + 18432, [512], F32)
            rstdv = var
            zt = [view(R_T + 20480 + i * 2048, [512], F32) for i in range(2)]
            for c in (0, 2, 1, 3):
                if c < 2:
                    S.op("dve", lambda c=c: nc.vector.tensor_tensor(
                        out=dwd[:, c, :, :], in0=ident_f.unsqueeze(1).to_broadcast([128, 31, 128]),
                        in1=cw[:, c, :].unsqueeze(2).to_broadcast([128, 31, 128]), op=ALU.mult),
                         reads=["ident_f", "cw"], writes=[("dwd", c)])
                else:
                    S.op("pool", lambda c=c: nc.gpsimd.tensor_tensor(
                        out=dwd[:, c, :, :], in0=ident_f.unsqueeze(1).to_broadcast([128, 31, 128]),
                        in1=cw[:, c, :].unsqueeze(2).to_broadcast([128, 31, 128]), op=ALU.mult),
                         reads=["ident_f", "cw"], writes=[("dwd", c)])
            E = view(R_T + 26624, [8, 640], BF16)
            etmp = [view(R_CAT + 17408 + i * 2560, [640], F32) for i in range(2)]
            maskT = view(R_CAT + 17408 + 5120, [640], F32)
            S.dma("sp", "maskT", lambda: nc.sync.dma_start(out=maskT, in_=maskT_d), writes=["maskT"])
            def build_E(h):
                et = etmp[h % 2]; ek = ("etmp", h % 2)
                S.dma("sp", "et%d" % (h % 2), lambda: nc.sync.dma_start(out=et, in_=biasT_d[h]), writes=[ek])
                S.op("dve", lambda: nc.vector.tensor_tensor(out=et, in0=et, in1=maskT, op=ALU.add), reads=[ek, "maskT"], writes=[ek])
                S.op("act", lambda: nc.scalar.activation(out=E[:, h, :], in_=et, func=AF.Exp), reads=[ek], writes=[("E", h)])
            e_pending = list(range(8))
            a3_hooks = []
            a2_order = [gi for gi, g in enumerate(groups) if g[2]]
            a2_order = a2_order[1:] + a2_order[:1]
            for gi in a2_order:
                t0, nt, full = groups[gi]
                N = nt * 128
                ct0 = (t0 - 4) * 128
                for c in range(4):
                    b = nbank()
                    for j in range(31):
                        S.mm(lambda j=j, b=b, c=c: nc.tensor.matmul(pf[b][:, :N], lhsT=dwd[:, c, j, :],
                                                                    rhs=hglu[:, c, ct0 + j:ct0 + j + N], start=(j == 0), stop=(j == 30)),
                             reads=[("dwd", c)] + [("hglu", c, g) for g in range(1, 6)] + ["hglu_pad"],
                             writes=[("pf", b)], last=(j == 30))
                    S.op("act", lambda b=b, c=c: nc.scalar.activation(out=cf[:, c, :N], in_=pf[b][:, :N], func=AF.Identity,
                                                                      bias=cvec[:, c, 0:1], scale=1.0),
                         reads=[("pf", b), "cvec"], writes=[("cf", c)])
                    S.op("act", lambda b=b, c=c: nc.scalar.activation(out=csq[:, c, :N], in_=pf[b][:, :N], func=AF.Square,
                                                                      bias=cvec[:, c, 0:1], scale=1.0),
                         reads=[("pf", b), "cvec"], writes=[("csq", c)])
                    S.op("dve", lambda c=c: nc.vector.tensor_copy(out=cb[:, c, :N], in_=cf[:, c, :N]),
                         reads=[("cf", c)], writes=[("cb", c)])
                    if e_pending:
                        build_E(e_pending.pop(0))
                b1 = nbank()
                for c in range(4):
                    S.mm(lambda c=c, b1=b1: nc.tensor.matmul(pf[b1][:, :N], lhsT=ones_b, rhs=cb[:, c, :N], start=(c == 0), stop=(c == 3)),
                         reads=["ones_b", ("cb", c)], writes=[("pf", b1)], last=(c == 3))
                b2 = nbank()
                for c in range(4):
                    S.mm(lambda c=c, b2=b2: nc.tensor.matmul(pf[b2][:, :N], lhsT=ones_b, rhs=csq[:, c, :N], start=(c == 0), stop=(c == 3)),
                         reads=["ones_b", ("csq", c)], writes=[("pf", b2)], last=(c == 3))
                S.op("act", lambda b1=b1: nc.scalar.mul(out=mean[:, :N], in_=pf[b1][:, :N], mul=1.0 / 512), reads=[("pf", b1)], writes=["mean"])
                S.op("dve", lambda: nc.vector.tensor_tensor(out=var[:, :N], in0=mean[:, :N], in1=mean[:, :N], op=ALU.mult),
                     reads=["mean"], writes=["var", "rstdv"])
                S.op("dve", lambda b2=b2: nc.vector.scalar_tensor_tensor(out=var[:, :N], in0=pf[b2][:, :N], scalar=1.0 / 512, in1=var[:, :N],
                                                                         op0=ALU.mult, op1=ALU.subtract),
                     reads=[("pf", b2), "var"], writes=["var"])
                S.op("dve", lambda: nc.vector.tensor_scalar(out=var[:, :N], in0=var[:, :N], scalar1=0.0, scalar2=EPS, op0=ALU.max, op1=ALU.add),
                     reads=["var"], writes=["var"])
                S.op("act", lambda: nc.scalar.activation(out=var[:, :N], in_=var[:, :N], func=AF.Ln), reads=["var"], writes=["var"])
                S.op("act", lambda: nc.scalar.activation(out=rstdv[:, :N], in_=var[:, :N], func=AF.Exp, scale=-0.5),
                     reads=["var"], writes=["var", "rstdv"])
                def norm_block(c, N=N, ct0=ct0, gi=gi):
                    z = zt[c % 2]; zk = ("zt", c % 2)
                    S.op("pool", lambda: nc.gpsimd.tensor_tensor(out=z[:, :N], in0=cf[:, c, :N], in1=mean[:, :N], op=ALU.subtract),
                         reads=[("cf", c), "mean"], writes=[zk])
                    S.op("dve", lambda: nc.vector.tensor_tensor(out=z[:, :N], in0=z[:, :N], in1=rstdv[:, :N], op=ALU.mult),
                         reads=[zk, "rstdv"], writes=[zk])
                    S.op("act", lambda: nc.scalar.activation(out=conv_outT[:, c, ct0:ct0 + N], in_=z[:, :N], func=AF.Silu,
                                                             scale=cvec[:, c, 1:2], bias=cvec[:, c, 2:3]),
                         reads=[zk, "cvec"], writes=[("coT", c, gi)])

                if gi == a2_order[-1]:
                    a3_hooks.append(lambda f=norm_block: [f(c) for c in range(4)])
                else:
                    for c in range(4):
                        norm_block(c)
            while e_pending:
                build_E(e_pending.pop(0))
            dump("coT", conv_outT, [4, QC], BF16, [("coT", c, g) for c in range(4) for g in range(1, 6)])

            pT_ring = [view(5120 + i * 2048, [2, 512], BF16) for i in range(3)]
            attn_tok = view(11264, [4, 512], BF16)
            rden = view(15360, [8], F32)
            W_out = view(R_T + 8192, [8, 1024], BF16)
            w_out_v = w_out_d.rearrange("(k p) c -> p k c", p=128)
            a2_keys = [("cf", c) for c in range(4)] + [("cb", c) for c in range(4)] + [("csq", c) for c in range(4)] + \
                      ["mean", "var", "rstdv", ("zt", 0), ("zt", 1)]
            wout_loaded = [False]

            def load_wout():
                if wout_loaded[0]:
                    return
                wout_loaded[0] = True
                for k in range(8):
                    S.dma("pool", "w_out%d" % k, lambda k=k: nc.gpsimd.dma_start(out=W_out[:, k, :], in_=w_out_v[:, k, :]),
                          writes=[("W_out", k)] + a2_keys)
            obank = [ptb[i][:, :].bitcast(F32) for i in range(2)]
            LA = 2
            QL = 32
            S.op("pool", lambda: nc.gpsimd.memset(attn_outT[:, :, 0:128 - QL], 0.0), writes=[("aoT", 4, "pad")])
            for gi, (n0, nt, full) in enumerate(groups):
                if not full:
                    continue
                if n0 == 4 and nt == 1:
                    kts = [4, 3, 2, 1, 0]
                    for hp in range(4):
                        si = hp
                        pp = PP[si % 3]
                        ppk = [("pf", 2 * (si % 3)), ("pf", 2 * (si % 3) + 1)]
                        pT = pT_ring[si % 3]; pk = ("pT", si % 3)
                        for hh in range(2):
                            p0 = hh * 64
                            for j, kt in enumerate(kts):
                                S.mm(lambda hh=hh, p0=p0, j=j, kt=kt: nc.tensor.matmul(
                                    pp[:, hh, j * QL:(j + 1) * QL], lhsT=kT[p0:p0 + 64, hp, kt * 128:(kt + 1) * 128],
                                    rhs=qT[p0:p0 + 64, hp, 128 - QL:128], start=(j == 0), stop=(j == 4), skip_group_check=True),
                                     reads=[("kT", hp, g) for g in range(6)] + [("qT", hp, g) for g in range(1, 6)],
                                     writes=ppk, last=(hh == 1 and j == 4))
                        S.op("act", lambda: nc.scalar.activation(out=pT[:, :, 0:5 * QL], in_=pp[:, :, 0:5 * QL], func=AF.Exp),
                             reads=ppk, writes=[pk])
                        ev = E[:, 2 * hp:2 * hp + 2, :].rearrange("p h (j x) -> p h j x", x=128)[:, :, :, 128 - QL:128]
                        S.op("dve", lambda: nc.vector.tensor_tensor(
                            out=pT[:, :, 0:5 * QL].rearrange("p h (j x) -> p h j x", x=QL),
                            in0=pT[:, :, 0:5 * QL].rearrange("p h (j x) -> p h j x", x=QL), in1=ev, op=ALU.mult),
                             reads=[pk, ("E", 2 * hp), ("E", 2 * hp + 1)], writes=[pk])
                        for hh in range(2):
                            h = 2 * hp + hh
                            for j, kt in enumerate(kts):
                                S.mm(lambda hh=hh, h=h, j=j, kt=kt: nc.tensor.matmul(
                                    obank[hh][0:QL, 0:65], lhsT=pT[:, hh, j * QL:(j + 1) * QL], rhs=V1[:, kt, h, :],
                                    start=(j == 0), stop=(j == 4), skip_group_check=True),
                                     reads=[pk, ("V1", kt), "V1ones"], writes=[("ptb", hh)], last=(j == 4))
                            rd = rden[0:QL, hh * 4:hh * 4 + 1]
                            rk = ("rden", hh)
                            S.op("dve", lambda hh=hh, rd=rd: nc.vector.reciprocal(out=rd, in_=obank[hh][0:QL, 64:65]),
                                 reads=[("ptb", hh)], writes=[rk])
                            S.op("dve", lambda hh=hh, rd=rd, h=h: nc.vector.tensor_tensor(
                                out=attn_tok[0:QL, 0, h * 64:(h + 1) * 64], in0=obank[hh][0:QL, 0:64],
                                in1=rd.to_broadcast([QL, 64]), op=ALU.mult),
                                 reads=[("ptb", hh), rk], writes=[("atok", h)])
                    pt = ptb[0]; ptk = ("ptb", 0)
                    for c in range(4):
                        S.mm(lambda c=c: nc.tensor.transpose(out=pt[:, c * QL:(c + 1) * QL], in_=attn_tok[0:QL, 0, c * 128:(c + 1) * 128],
                                                             identity=ident_b[0:QL, 0:QL]),
                             reads=[("atok", h_) for h_ in range(8)] + ["ident_b"], writes=[ptk], last=(c == 3))
                    S.op("act", lambda: nc.scalar.copy(out=attn_outT[:, :, 128 - QL:128],
                                                       in_=pt[:, 0:4 * QL].rearrange("p (c t) -> p c t", c=4)),
                         reads=[ptk, ("aoT", 4, "pad")], writes=[("aoT", 4)])
                    continue
                n1 = n0 + nt - 1
                steps = [(hp, kt) for hp in range(4) for kt in range(n0 - 4, n1 + 1)]
                info = {}

                def emit_score(si):
                    hp, kt = steps[si]
                    a, bq = max(kt, n0), min(kt + 4, n1)
                    Nq = (bq - a + 1) * 128
                    pp = PP[si % 3]
                    for hh in range(2):
                        p0 = hh * 64
                        S.mm(lambda hh=hh, p0=p0: nc.tensor.matmul(pp[:, hh, :Nq], lhsT=kT[p0:p0 + 64, hp, kt * 128:(kt + 1) * 128],
                                                                   rhs=qT[p0:p0 + 64, hp, (a - 4) * 128:(bq - 3) * 128],
                                                                   start=True, stop=True),
                             reads=[("kT", hp, g) for g in range(6)] + [("qT", hp, g) for g in range(1, 6)],
                             writes=[("pf", 2 * (si % 3)), ("pf", 2 * (si % 3) + 1)], last=(hh == 1))
                    info[si] = (a, bq, Nq)

                for si in range(min(LA, len(steps))):
                    emit_score(si)
                for si, (hp, kt) in enumerate(steps):
                    if si + LA < len(steps):
                        emit_score(si + LA)
                    a, bq, Nq = info[si]
                    if a3_hooks and si >= 2:
                        a3_hooks.pop(0)()
                    elif not a3_hooks:
                        load_wout()
                    pp = PP[si % 3]
                    ppk = [("pf", 2 * (si % 3)), ("pf", 2 * (si % 3) + 1)]
                    pT = pT_ring[si % 3]; pk = ("pT", si % 3)
                    S.op("act", lambda: nc.scalar.activation(out=pT[:, :, :Nq], in_=pp[:, :, :Nq], func=AF.Exp),
                         reads=ppk, writes=[pk])
                    S.op("dve", lambda: nc.vector.tensor_tensor(out=pT[:, :, :Nq], in0=pT[:, :, :Nq],
                                                                in1=E[:, 2 * hp:2 * hp + 2, (a - kt) * 128:(bq - kt + 1) * 128], op=ALU.mult),
                         reads=[pk, ("E", 2 * hp), ("E", 2 * hp + 1)], writes=[pk])
                    for hh in range(2):
                        h = 2 * hp + hh
                        for n in range(a, bq + 1):
                            first = (kt == n0 - 4 and n == a)
                            lastmm = (kt == n1 and n == bq)
                            S.mm(lambda n=n, first=first, lastmm=lastmm, hh=hh, h=h: nc.tensor.matmul(
                                obank[hh][:, (n - n0) * 65:(n - n0) * 65 + 65], lhsT=pT[:, hh, (n - a) * 128:(n - a + 1) * 128],
                                rhs=V1[:, kt, h, :], start=first, stop=lastmm, skip_group_check=True),
                                 reads=[pk, ("V1", kt), "V1ones"], writes=[("ptb", hh)], last=(n == bq))
                    if kt == n1:
                        ov2 = ptb2[:, :, :].bitcast(F32)[:, :, 0:nt * 65].rearrange("p h (n d) -> p h n d", n=nt)
                        rd2 = rden[:, 0:8].rearrange("p (h n) -> p h n", h=2)[:, :, 0:nt]
                        S.op("dve", lambda: nc.vector.reciprocal(out=rd2.unsqueeze(3), in_=ov2[:, :, :, 64:65]),
                             reads=[("ptb", 0), ("ptb", 1)], writes=["rden2"])
                        S.op("dve", lambda: nc.vector.tensor_tensor(
                            out=attn_tok[:, 0:nt, 2 * hp * 64:(2 * hp + 2) * 64].rearrange("p n (h d) -> p h n d", h=2),
                            in0=ov2[:, :, :, 0:64], in1=rd2.unsqueeze(3).to_broadcast([128, 2, nt, 64]), op=ALU.mult),
                             reads=[("ptb", 0), ("ptb", 1), "rden2"], writes=[("atok", 2 * hp), ("atok", 2 * hp + 1)])
                for n in range(n0, n1 + 1):
                    pt = ptb[n % 2]; ptk = ("ptb", n % 2)
                    for c in range(4):
                        S.mm(lambda c=c, n=n, pt=pt: nc.tensor.transpose(out=pt[:, c * 128:(c + 1) * 128],
                                                                         in_=attn_tok[:, n - n0, c * 128:(c + 1) * 128], identity=ident_b),
                             reads=[("atok", h_) for h_ in range(8)] + ["ident_b"], writes=[ptk], last=(c == 3))
                    S.op("dve", lambda n=n, pt=pt: nc.vector.tensor_copy(out=attn_outT[:, :, (n - 4) * 128:(n - 3) * 128],
                                                                         in_=pt[:, 0:512].rearrange("p (c t) -> p c t", c=4)),
                         reads=[ptk], writes=[("aoT", n)])
            dump("aoT", attn_outT, [4, QC], BF16, [("aoT", n) for n in range(4, NT)])
            S.fence()

            for i in range(1, 4):
                S.dma("act", "c_g%d" % i, lambda i=i: nc.scalar.dma_start(out=gb[i], in_=gvec_d[i:i + 1, :].partition_broadcast(128)),
                      writes=[("gb", i)])
            xt_ring = [view(R_KQV + 16384 + i * 4096, [1024], F32) for i in range(3)]
            h1_ring = [view(R_KQV + 28672 + i * 4096, [1024], F32) for i in range(4)]
            ub_ring = [view(R_KQV + 45056 + i * 2048, [1024], BF16) for i in range(2)]
            junk = view(R_KQV + 49152, [1024], BF16)
            S.op("pool", lambda: nc.gpsimd.memset(u2T[:, :, 2050:2052], 0.0), writes=["u2T_pad"])
            O_WD = R_CAT
            O_ACT = O_WD + 45056
            O_WR = O_ACT + 45232
            NRING = 8
            PF = 6
            wring = [view(O_WR + i * 2048, [8, 128], BF16) for i in range(NRING)]
            w_up_v = w_up_d.rearrange("(k p) c -> p k c", p=128)
            seq = []
            for H in range(2):
                for j in range(22):
                    seq.append((H, j, 0))
                    seq.append((H, j, 1))

            def load_wup(si):
                H, j, isval = seq[si]
                cbk = (22 + j) if isval else j
                slot = si % NRING
                S.dma("pool", "wup%d" % slot, lambda: nc.gpsimd.dma_start(out=wring[slot], in_=w_up_v[:, :, cbk * 128:(cbk + 1) * 128]),
                      writes=[("wring", slot)])

            for si in range(PF):
                load_wup(si)

            def a4_s0(i):
                w = i + 4
                xt = xt_ring[i % 3]
                S.dma("sp", "xt4%d" % (i % 3), lambda: nc.sync.dma_start(out=xt, in_=xw[w * 128:(w + 1) * 128, :]),
                      writes=[("xt4", i % 3)])
                pp = PP[i % 3]
                for half in range(2):
                    for k in range(8):
                        src = conv_outT if k < 4 else attn_outT
                        S.mm(lambda k=k, half=half, src=src: nc.tensor.matmul(
                            pp[:, half, :], lhsT=src[:, k % 4, i * 128:(i + 1) * 128],
                            rhs=W_out[:, k, half * 512:(half + 1) * 512], start=(k == 0), stop=(k == 7)),
                             reads=[("W_out", k)] + ([("coT", k, g) for g in range(1, 6)] if k < 4 else [("aoT", w), ("aoT", 4, "pad")]),
                             writes=[("pf", 2 * (i % 3) + half)], last=(k == 7))

            def a4_s1(i):
                pp = PP[i % 3]
                xt = xt_ring[i % 3]; xk = ("xt4", i % 3)
                h1 = h1_ring[i % 4]; hk = ("h1", i % 4)
                c_sum, c_r = newstat(), newstat()
                S.op("act", lambda: nc.scalar.activation(out=junk.rearrange("p (a b) -> p a b", a=2), in_=pp[:, :, :], func=AF.Square,
                                                         accum_out=stat[:, c_sum:c_sum + 1]),
                     reads=[("pf", 2 * (i % 3)), ("pf", 2 * (i % 3) + 1)], writes=["junk4", ("st", c_sum)])
                rstd_act(c_sum, c_r, D, [("st", c_sum)])
                for half in range(2):
                    S.op("dve", lambda half=half: nc.vector.scalar_tensor_tensor(
                        out=h1[:, half * 512:(half + 1) * 512], in0=pp[:, half, :], scalar=stat[:, c_r:c_r + 1],
                        in1=gb[1][:, half * 512:(half + 1) * 512], op0=ALU.mult, op1=ALU.mult),
                         reads=[("pf", 2 * (i % 3) + half), ("st", c_r), ("gb", 1)], writes=[hk])
                S.op("dve", lambda: nc.vector.tensor_tensor(out=h1, in0=h1, in1=xt, op=ALU.add),
                     reads=[hk, xk], writes=[hk])

            def a4_s2(i):
                h1 = h1_ring[i % 4]; hk = ("h1", i % 4)
                ub = ub_ring[i % 2]; ubk = ("ub4", i % 2)
                if i >= 1:
                    S.dma("sp", "h1w%d" % (i % 4), lambda: nc.sync.dma_start(out=h1s[(i - 1) * 128:i * 128, :], in_=h1),
                          reads=[hk], writes=[("h1s", i - 1)])
                c2, c2r = newstat(), newstat()
                S.op("act", lambda: nc.scalar.activation(out=junk, in_=h1, func=AF.Square, accum_out=stat[:, c2:c2 + 1]),
                     reads=[hk], writes=["junk4", ("st", c2)])
                rstd_act(c2, c2r, D, [("st", c2)])
                S.op("dve", lambda: nc.vector.scalar_tensor_tensor(out=ub, in0=h1, scalar=stat[:, c2r:c2r + 1], in1=gb[2],
                                                                   op0=ALU.mult, op1=ALU.mult),
                     reads=[hk, ("st", c2r), ("gb", 2)], writes=[ubk])

            def a4_s2b(i):
                ub = ub_ring[i % 2]; ubk = ("ub4", i % 2)
                pt = ptb[i % 2]; ptk = ("ptb", i % 2)
                for k in range(8):
                    S.mm(lambda k=k: nc.tensor.transpose(out=pt[:, k * 128:(k + 1) * 128], in_=ub[:, k * 128:(k + 1) * 128],
                                                         identity=ident_b),
                         reads=[ubk, "ident_b"], writes=[ptk], last=(k == 7))

            def a4_s3(i):
                pt = ptb[i % 2]; ptk = ("ptb", i % 2)
                ptv = pt[:, :].rearrange("p (k t) -> p k t", k=8)
                if i >= 1:
                    S.op("act", lambda: nc.scalar.copy(out=u2T[:, :, 2 + (i - 1) * 128:2 + i * 128], in_=ptv),
                         reads=[ptk], writes=[("u2T", i)])
                else:
                    S.op("act", lambda: nc.scalar.activation(out=u2T[:, :, 0:2], in_=ptv[:, :, 126:128], func=AF.Identity,
                                                             scale=flag[:, 0:1]),
                         reads=[ptk, "flag"], writes=[("u2T", 0)])

            for t in range(NQ + 5):
                if 0 <= t - 5 < NQ:
                    a4_s3(t - 5)
                if 0 <= t - 4 < NQ:
                    a4_s2b(t - 4)
                if 0 <= t - 3 < NQ:
                    a4_s2(t - 3)
                if 0 <= t - 1 < NQ:
                    a4_s1(t - 1)
                if t < NQ:
                    a4_s0(t)
            dump("u2T", u2T, [8, 2052], BF16, [("u2T", i) for i in range(NQ)] + ["u2T_pad"])
            S.fence()

        with ExitStack() as psB:
            PU = [psB.enter_context(nc.psum_tensor("pu%d" % i, [128, 3, 512], F32)) for i in range(2)]
            PD = psB.enter_context(nc.psum_tensor("pd", [128, 1024], F32))
            o = R_CAT
            W_down = view(o, [22, 1024], BF16); o += 45056
            act = view(o, [22, 1028], BF16); o += 45232
            assert o == O_WR
            o += NRING * 2048
            tA = [view(o + i * 4104, [3, 342], F32) for i in range(3)]; o += 3 * 4104
            tB = [view(o, [3, 342], F32)]; o += 4104
            h1r = [view(o + i * 4096, [1024], F32) for i in range(2)]; o += 8192
            ot = [view(o + i * 4096, [1024], F32) for i in range(2)]; o += 8192
            junk = view(o, [1024], BF16); o += 2048
            xsb = [view(o + i * 4128, [3, 344], F32) for i in range(2)]; o += 2 * 4128
            assert o <= 204800, o
            w_down_v = w_down_d.rearrange("(f p) c -> p f c", p=128)

            def emit_wdown(H):
                b0 = 1024 * H
                for i in range(8):
                    tile = H * 8 + i
                    if i % 2 == 0:
                        pdv, pdk = PD[:, :], "PD"
                    else:
                        pdv, pdk = PU[1][:, 0:2, :].rearrange("p a b -> p (a b)"), ("PU", 1)
                    for half in range(2):
                        for f in range(22):
                            S.mm(lambda f=f, half=half, i=i: nc.tensor.matmul(
                                pdv[:, half * 512:(half + 1) * 512], lhsT=act[:, f, i * 128:(i + 1) * 128],
                                rhs=W_down[:, f, half * 512:(half + 1) * 512], start=(f == 0), stop=(f == 21)),
                                 reads=[("act", f), ("W_down", f)], writes=[pdk], last=(f == 21))
                    hr = h1r[tile % 2]; hrk = ("h1r", tile % 2)
                    S.dma("sp", "h1r%d" % (tile % 2), lambda tile=tile, hr=hr: nc.sync.dma_start(out=hr, in_=h1s[tile * 128:(tile + 1) * 128, :]),
                          reads=[("h1s", tile)], writes=[hrk])
                    c_s, c_r = newstat(), newstat()
                    S.op("act", lambda c_s=c_s: nc.scalar.activation(out=junk, in_=pdv, func=AF.Square, accum_out=stat[:, c_s:c_s + 1]),
                         reads=[pdk], writes=["junkB", ("st", c_s)])
                    rstd_from(c_s, c_r, D, [("st", c_s)])
                    oo = ot[tile % 2]; ok = ("ot", tile % 2)
                    S.op("dve", lambda oo=oo, c_r=c_r: nc.vector.scalar_tensor_tensor(out=oo, in0=pdv, scalar=stat[:, c_r:c_r + 1], in1=gb[3],
                                                                                     op0=ALU.mult, op1=ALU.mult),
                         reads=[pdk, ("st", c_r), ("gb", 3)], writes=[ok])
                    S.op("dve", lambda oo=oo, hr=hr: nc.vector.tensor_tensor(out=oo, in0=oo, in1=hr, op=ALU.add),
                         reads=[ok, hrk], writes=[ok])
                    S.dma("sp", "ow%d" % (tile % 2), lambda tile=tile, oo=oo: nc.sync.dma_start(out=out_d[tile * 128:(tile + 1) * 128, :], in_=oo),
                          reads=[ok], writes=[("out", tile)])

            pending_tail = [None]
            for si, (H, j, isval) in enumerate(seq):
                if si + PF < len(seq):
                    load_wup(si + PF)
                if si < 22:
                    S.dma("pool", "w_down%d" % si, lambda f=si: nc.gpsimd.dma_start(out=W_down[:, f, :], in_=w_down_v[:, f, :]),
                          writes=[("W_down", si)])
                b0 = 1024 * H
                cbk = (22 + j) if isval else j
                slot = si % NRING
                pu = PU[si % 2]; puk = ("PU", si % 2)
                for k in range(8):
                    for v in range(3):
                        S.mm(lambda k=k, v=v, pu=pu: nc.tensor.matmul(pu[:, v, 0:344], lhsT=wring[slot][:, k, :],
                                                                      rhs=u2T[:, k, b0 + 342 * v:b0 + 342 * v + 344],
                                                                      start=(k == 0), stop=(k == 7)),
                             reads=[("wring", slot)] + [("u2T", i_) for i_ in range(NQ)] + ["u2T_pad"],
                             writes=[puk], last=(k == 7 and v == 2))
                ta = tA[si % 3]; tak = ("tA", si % 3)
                tb = tB[0]; tbk = ("tB", 0)
                xs = xsb[si % 2]; xsk = ("xs", si % 2)
                S.op("act", lambda: nc.scalar.copy(out=xs, in_=pu[:, :, 0:344]), reads=[puk], writes=[xsk])
                S.op("act", lambda: nc.scalar.activation(out=ta, in_=pu[:, :, 2:344], func=AF.Identity,
                                                         scale=fdw[:, cbk, 2:3], bias=fdw[:, cbk, 3:4]),
                     reads=[puk, "fdw"], writes=[tak])
                S.op("dve", lambda: nc.vector.scalar_tensor_tensor(out=tb, in0=xs[:, :, 1:343], scalar=fdw[:, cbk, 1:2],
                                                                   in1=ta, op0=ALU.mult, op1=ALU.add),
                     reads=[xsk, "fdw", tak], writes=[tbk])
                S.op("dve", lambda: nc.vector.scalar_tensor_tensor(out=ta, in0=xs[:, :, 0:342], scalar=fdw[:, cbk, 0:1],
                                                                   in1=tb, op0=ALU.mult, op1=ALU.add),
                     reads=[xsk, "fdw", tbk], writes=[tak])
                av = act[:, j, 0:1026].rearrange("p (v m) -> p v m", v=3)
                if pending_tail[0] is not None:
                    pending_tail[0]()
                if not isval:
                    def tail(av=av, ta=ta, tak=tak, j=j):
                        S.op("act", lambda: nc.scalar.activation(out=av, in_=ta, func=AF.Gelu_apprx_tanh),
                             reads=[tak], writes=[("act", j)])
                else:
                    def tail(av=av, ta=ta, tak=tak, j=j):
                        S.op("dve", lambda: nc.vector.tensor_tensor(out=av, in0=av, in1=ta, op=ALU.mult),
                             reads=[tak, ("act", j)], writes=[("act", j)])
                pending_tail[0] = tail
                if j == 21 and isval:
                    pending_tail[0]()
                    pending_tail[0] = None
                    emit_wdown(H)

            for slot, ds in S.dsem.items():
                if slot.startswith("ow") or slot.startswith("dbg_"):
                    nc.sync.wait_ge(ds[0], ds[1])
        build_program.ninst = dict(S.ninst)
    return nc, dbg


_CACHE = {}


def _prep_shared(inp):
    f32 = np.float32
    rel = np.asarray(inp["rel_bias"], f32)[0]
    ki = np.arange(128)[:, None]
    qi = np.arange(640)[None, :]
    idx = np.clip(qi - ki, -128, 128) + 128
    biasT = np.ascontiguousarray(rel[:, idx])
    dc = qi // 64 - ki // 64
    maskT = np.where((dc >= 0) & (dc <= 8), 0.0, -30000.0).astype(f32)
    cw = np.ascontiguousarray(np.asarray(inp["conv_dw_w"], f32)[0].T.reshape(4, 128, 31).transpose(1, 0, 2)).reshape(128, 124)
    cv = np.stack([np.asarray(inp["conv_dw_b"], f32)[0], np.asarray(inp["conv_ln_g"], f32)[0],
                   np.asarray(inp["conv_ln_b"], f32)[0]])
    cvec = np.ascontiguousarray(cv.reshape(3, 4, 128).transpose(2, 1, 0)).reshape(128, 12)
    fd = np.concatenate([np.asarray(inp["ffn_dw_w"], f32)[0], np.asarray(inp["ffn_dw_b"], f32)], axis=0)
    fdw = np.ascontiguousarray(fd.reshape(4, 44, 128).transpose(2, 1, 0)).reshape(128, 176)
    gvec = np.ascontiguousarray(np.concatenate([np.asarray(inp[k], f32) for k in
                                                ("norm_mix_pre", "norm_mix_post", "norm_ffn_pre", "norm_ffn_post")], axis=0))
    return {
        "w_in": np.ascontiguousarray(np.asarray(inp["w_in"], f32)[0]),
        "w_out": np.ascontiguousarray(np.asarray(inp["w_out"], f32)[0]),
        "w_up": np.ascontiguousarray(np.asarray(inp["w_up"], f32)[0]),
        "w_down": np.ascontiguousarray(np.asarray(inp["w_down"], f32)[0]),
        "gvec": gvec, "cw": cw, "cvec": cvec, "fdw": fdw, "biasT": biasT, "maskT": maskT,
        "ident": np.eye(128, dtype=f32),
    }


def kernel(**inputs):
    x = np.asarray(inputs["x"], np.float32)
    shared = _prep_shared(inputs)
    in_maps = []
    for c in range(8):
        b, s = c // 4, (c % 4) * OWN
        xwin = np.zeros((NT * 128, D), np.float32)
        lo = s - 640
        src_lo = max(lo, 0)
        xwin[src_lo - lo:, :] = x[b, src_lo:s + OWN, :]
        m = dict(shared)
        m["xw"] = xwin
        m["flag"] = np.full((128, 1), 1.0 if s > 0 else 0.0, np.float32)
        in_maps.append(m)
    if "nc" not in _CACHE:
        _CACHE["nc"] = build_program()
    nc, dbg = _CACHE["nc"]
    res = run_bass_kernel_spmd(nc, in_maps, core_ids=list(range(8)))
    kernel.last_results = res
    out = np.empty((2, 8192, D), np.float32)
    for c in range(8):
        b, s = c // 4, (c % 4) * OWN
        out[b, s:s + OWN, :] = res.results[c]["out"]
    return out
```
